# Optimizing a Trainium2 kernel written in Bass

```python
import math
import jax, jax.numpy as jnp
from jax import lax
import numpy as np

D_MODEL = 1024
BATCH = 4
SEQ = 8192
DEPTH = 1

HEAD_DIM = 128
MOBA_HEADS = 6
MOBA_BLOCK = 256
MOBA_TOPK = 3
MOBA_QCHUNK = 32
GMLP_GROUPS = 6
GMLP_GROUP_DIM = 128
GMLP_CHUNK = 128
XATTN_HEADS = 4
MEM_LEN = 256
ROPE_THETA = 500000.0
ROT_DIM = HEAD_DIM // 4
EPS = 1e-6

A_WIDTH = MOBA_HEADS * HEAD_DIM
B_WIDTH = GMLP_GROUPS * GMLP_GROUP_DIM
C_WIDTH = XATTN_HEADS * HEAD_DIM
IN_SPLITS = (A_WIDTH, A_WIDTH, A_WIDTH, A_WIDTH, B_WIDTH, B_WIDTH, B_WIDTH, C_WIDTH, C_WIDTH, D_MODEL, D_MODEL, D_MODEL)
IN_WIDTH = 4 * A_WIDTH + 3 * B_WIDTH + 2 * C_WIDTH + 3 * D_MODEL

kernel_name = "hybrid_moba_gmlp_xattn_gated_layer"


def rms_norm(x, g):
    xf = x.astype(jnp.float32)
    y = xf * lax.rsqrt(jnp.mean(xf * xf, axis=-1, keepdims=True) + EPS)
    return (y * g.astype(jnp.float32)).astype(x.dtype)


def layer_norm(x, g):
    xf = x.astype(jnp.float32)
    mu = jnp.mean(xf, axis=-1, keepdims=True)
    var = jnp.mean(jnp.square(xf - mu), axis=-1, keepdims=True)
    return ((xf - mu) * lax.rsqrt(var + EPS) * g.astype(jnp.float32)).astype(x.dtype)


def split_cols(t, widths):
    offs, acc = [], 0
    for w in widths[:-1]:
        acc += w
        offs.append(acc)
    return jnp.split(t, offs, axis=-1)


def partial_rope(x, pos):
    half = ROT_DIM // 2
    inv_freq = ROPE_THETA ** (-jnp.arange(0, ROT_DIM, 2, dtype=jnp.float32) / ROT_DIM)
    ang = pos.astype(jnp.float32)[:, None] * inv_freq[None, :]
    cos = jnp.cos(ang)[None, :, None, :].astype(x.dtype)
    sin = jnp.sin(ang)[None, :, None, :].astype(x.dtype)
    x1, x2, rest = x[..., :half], x[..., half:ROT_DIM], x[..., ROT_DIM:]
    return jnp.concatenate([x1 * cos - x2 * sin, x2 * cos + x1 * sin, rest], axis=-1)


def moba_attention(q, k, v):
    B, S, H, dh = q.shape
    n_blocks = -(-S // MOBA_BLOCK)
    k_sel_n = min(MOBA_TOPK, n_blocks)
    pad = n_blocks * MOBA_BLOCK - S
    scale = HEAD_DIM ** -0.5
    qh = jnp.transpose(q, (0, 2, 1, 3))
    kh = jnp.pad(jnp.transpose(k, (0, 2, 1, 3)), ((0, 0), (0, 0), (0, pad), (0, 0)))
    vh = jnp.pad(jnp.transpose(v, (0, 2, 1, 3)), ((0, 0), (0, 0), (0, pad), (0, 0)))
    k_blk = kh.reshape(B, H, n_blocks, MOBA_BLOCK, dh)
    v_blk = vh.reshape(B, H, n_blocks, MOBA_BLOCK, dh)
    k_mean = jnp.mean(k_blk.astype(jnp.float32), axis=3)
    b_ix = jnp.arange(B)[:, None, None, None]
    h_ix = jnp.arange(H)[None, :, None, None]
    blk_ids = jnp.arange(n_blocks)

    def chunk_fn(c):
        start = c * MOBA_QCHUNK
        qc = lax.dynamic_slice_in_dim(qh, start, MOBA_QCHUNK, axis=2)
        pos_q = start + jnp.arange(MOBA_QCHUNK)
        own = start // MOBA_BLOCK
        bs = jnp.einsum('bhqd,bhnd->bhqn', qc.astype(jnp.float32), k_mean)
        bs = jnp.where((blk_ids < own)[None, None, None, :], bs, -jnp.inf)
        _, sel_idx = lax.top_k(bs, k_sel_n)
        sel_valid = sel_idx < own
        k_sel = k_blk[b_ix, h_ix, sel_idx]
        v_sel = v_blk[b_ix, h_ix, sel_idx]
        lg_sel = jnp.einsum('bhqd,bhqkpd->bhqkp', qc, k_sel).astype(jnp.float32) * scale
        lg_sel = jnp.where(sel_valid[..., None], lg_sel, -jnp.inf)
        lg_sel = lg_sel.reshape(B, H, MOBA_QCHUNK, k_sel_n * MOBA_BLOCK)
        k_own = lax.dynamic_index_in_dim(k_blk, own, axis=2, keepdims=False)
        v_own = lax.dynamic_index_in_dim(v_blk, own, axis=2, keepdims=False)
        key_pos = own * MOBA_BLOCK + jnp.arange(MOBA_BLOCK)
        causal = key_pos[None, :] <= pos_q[:, None]
        lg_own = jnp.einsum('bhqd,bhpd->bhqp', qc, k_own).astype(jnp.float32) * scale
        lg_own = jnp.where(causal[None, None], lg_own, -jnp.inf)
        p = jax.nn.softmax(jnp.concatenate([lg_sel, lg_own], axis=-1), axis=-1).astype(v.dtype)
        p_sel = p[..., :k_sel_n * MOBA_BLOCK].reshape(B, H, MOBA_QCHUNK, k_sel_n, MOBA_BLOCK)
        p_own = p[..., k_sel_n * MOBA_BLOCK:]
        return (jnp.einsum('bhqkp,bhqkpd->bhqd', p_sel, v_sel)
                + jnp.einsum('bhqp,bhpd->bhqd', p_own, v_own))

    out = lax.map(chunk_fn, jnp.arange(S // MOBA_QCHUNK))
    return jnp.transpose(out, (1, 0, 3, 2, 4)).reshape(B, S, H * dh)


def gmlp_spatial_gate(u, v, ln_g, w_spatial, b_spatial):
    B, S, _ = v.shape
    vn = layer_norm(v, ln_g).reshape(B, S // GMLP_CHUNK, GMLP_CHUNK, GMLP_GROUPS, GMLP_GROUP_DIM)
    w_causal = jnp.tril(w_spatial)
    mixed = jnp.einsum('gts,bnsgd->bntgd', w_causal, vn) + jnp.transpose(b_spatial)[None, None, :, :, None]
    return u * mixed.reshape(B, S, B_WIDTH)


def memory_cross_attention(q, mem_n, w_mem_kv):
    B, S, _ = q.shape
    M = mem_n.shape[1]
    mk, mv = jnp.split(mem_n @ w_mem_kv, 2, axis=-1)
    qh = q.reshape(B, S, XATTN_HEADS, HEAD_DIM)
    kh = mk.reshape(B, M, XATTN_HEADS, HEAD_DIM)
    vh = mv.reshape(B, M, XATTN_HEADS, HEAD_DIM)
    lg = jnp.einsum('bshd,bmhd->bhsm', qh, kh).astype(jnp.float32) * (HEAD_DIM ** -0.5)
    p = jax.nn.softmax(lg, axis=-1).astype(q.dtype)
    return jnp.einsum('bhsm,bmhd->bshd', p, vh).reshape(B, S, C_WIDTH)


def setup_inputs(seed: int = 0) -> dict:
    key = jax.random.key(seed)
    ks = jax.random.split(key, 16)
    f32 = jnp.float32

    def nrm(k, shape, fan_in):
        return jax.random.normal(k, shape, f32) * (fan_in ** -0.5)

    def gain(k, shape):
        return 1.0 + 0.1 * jax.random.normal(k, shape, f32)

    return {
        "x": jax.random.normal(ks[0], (BATCH, SEQ, D_MODEL), f32),
        "mem": jax.random.normal(ks[1], (BATCH, MEM_LEN, D_MODEL), f32),
        "norm_g": gain(ks[2], (DEPTH, D_MODEL)),
        "mem_norm_g": gain(ks[3], (DEPTH, D_MODEL)),
        "final_norm_g": gain(ks[4], (D_MODEL,)),
        "w_in": nrm(ks[5], (DEPTH, D_MODEL, IN_WIDTH), D_MODEL),
        "w_mem_kv": nrm(ks[6], (DEPTH, D_MODEL, 2 * C_WIDTH), D_MODEL),
        "gmlp_ln_g": gain(ks[7], (DEPTH, B_WIDTH)),
        "w_spatial": nrm(ks[8], (DEPTH, GMLP_GROUPS, GMLP_CHUNK, GMLP_CHUNK), GMLP_CHUNK),
        "b_spatial": gain(ks[9], (DEPTH, GMLP_GROUPS, GMLP_CHUNK)),
        "w_branch_a": nrm(ks[10], (DEPTH, A_WIDTH, D_MODEL), A_WIDTH),
        "w_branch_b": nrm(ks[11], (DEPTH, B_WIDTH, D_MODEL), B_WIDTH),
        "w_branch_c": nrm(ks[12], (DEPTH, C_WIDTH, D_MODEL), C_WIDTH),
        "w_out": nrm(ks[13], (DEPTH, D_MODEL, D_MODEL), D_MODEL),
    }


def reference(x, mem, norm_g, mem_norm_g, final_norm_g, w_in, w_mem_kv, gmlp_ln_g,
              w_spatial, b_spatial, w_branch_a, w_branch_b, w_branch_c, w_out):
    B, S, _ = x.shape
    pos = jnp.arange(S)
    for layer in range(DEPTH):
        h = rms_norm(x, norm_g[layer])
        (qa, ka, va, ga, ub, vb, gb, qc, gc, ma, mb, mc) = split_cols(h @ w_in[layer], IN_SPLITS)
        qa = partial_rope(qa.reshape(B, S, MOBA_HEADS, HEAD_DIM), pos)
        ka = partial_rope(ka.reshape(B, S, MOBA_HEADS, HEAD_DIM), pos)
        va = va.reshape(B, S, MOBA_HEADS, HEAD_DIM)
        out_a = moba_attention(qa, ka, va) * jax.nn.silu(ga)
        out_b = gmlp_spatial_gate(ub, vb, gmlp_ln_g[layer], w_spatial[layer], b_spatial[layer]) * jax.nn.silu(gb)
        mem_n = rms_norm(mem, mem_norm_g[layer])
        out_c = memory_cross_attention(qc, mem_n, w_mem_kv[layer]) * jax.nn.silu(gc)
        y = (jax.nn.sigmoid(ma) * (out_a @ w_branch_a[layer])
             + jax.nn.sigmoid(mb) * (out_b @ w_branch_b[layer])
             + jax.nn.sigmoid(mc) * (out_c @ w_branch_c[layer]))
        x = x + y @ w_out[layer]
    return rms_norm(x, final_norm_g)
```

```python
import contextlib
import numpy as np
import concourse.bass as bass
import concourse.mybir as mybir
from concourse.bass_utils import run_bass_kernel_spmd

F32 = mybir.dt.float32
BF16 = mybir.dt.bfloat16
AF = mybir.ActivationFunctionType
ALU = mybir.AluOpType
AX = mybir.AxisListType

D = 1024
S_LEN = 8192
NBLK = 32
H = 6
HC = 4
MEM = 256
INW = 9472
NG = 8
NCG = 16
OFF = dict(qa=0, ka=768, va=1536, ga=2304, ub=3072, vb=3840, gb=4608, qc=5376, gc=5888,
           ma=6400, mb=7424, mc=8448)
NEG = -30000.0
SCALE = 128 ** -0.5
EPS = 1e-6
WBUF = 6144
_DBG = dict(ncg=NCG, ng=NG, k=True, v=True, cores=8, fin=True, later=True, t1=True)


class Buf:
    def __init__(self, name, excl=False):
        self.name = name
        self.excl = excl
        self.last_w = None
        self.readers = []
        self.sem = None
        self.dma_count = 0


class Op:
    __slots__ = ("eng", "emit", "deps", "is_dma", "sem", "ticket", "need_signal")

    def __init__(self, eng, emit, is_dma):
        self.eng = eng
        self.emit = emit
        self.deps = []
        self.is_dma = is_dma
        self.sem = None
        self.ticket = None
        self.need_signal = False


class Sched:
    ENGS = ("pe", "act", "dve", "pool", "sp")

    def __init__(self, nc):
        self.nc = nc
        self.lists = {e: [] for e in self.ENGS}
        self.eng_sem = {}

    def _track(self, op, reads, writes):
        xr = [b for b in reads if b.excl]
        if xr:
            reads = [b for b in reads if not b.excl]
            writes = list(writes) + xr
        deps = []
        for b in reads:
            if b.last_w is not None:
                deps.append(b.last_w)
        for b in writes:
            if b.last_w is not None:
                deps.append(b.last_w)
            deps.extend(b.readers)
        seen = set()
        for d in deps:
            if d is op or id(d) in seen:
                continue
            seen.add(id(d))
            if (not d.is_dma) and (not op.is_dma) and d.eng == op.eng and op.eng == "pe":
                continue
            op.deps.append(d)
            if not d.is_dma:
                d.need_signal = True
        for b in reads:
            b.readers.append(op)
        for b in writes:
            b.last_w = op
            b.readers = []

    def op(self, eng, emit, reads=(), writes=()):
        o = Op(eng, emit, False)
        self._track(o, reads, writes)
        self.lists[eng].append(o)
        return o

    def dma(self, queue, emit, reads=(), writes=(), sem_buf=None):
        o = Op(queue, emit, True)
        sb = sem_buf if sem_buf is not None else writes[0]
        if sb.sem is None:
            sb.sem = self.nc.alloc_semaphore(name="d_" + sb.name)
        sb.dma_count += 16
        o.sem = sb.sem
        o.ticket = sb.dma_count
        self._track(o, reads, writes)
        self.lists[queue].append(o)
        return o

    def finalize(self, final_wait_ops=()):
        nc = self.nc
        for e in self.ENGS:
            self.eng_sem[e] = nc.alloc_semaphore(name="e_" + e)
        for e in self.ENGS:
            cnt = 0
            for o in self.lists[e]:
                if not o.is_dma and o.need_signal:
                    cnt += 1
                    o.sem = self.eng_sem[e]
                    o.ticket = cnt
        lists = self.lists
        final_wait_ops = list(final_wait_ops)

        def run(e, eng):
            waited = {}
            for o in lists[e]:
                need = {}
                for d in o.deps:
                    key = id(d.sem)
                    if waited.get(key, 0) >= d.ticket:
                        continue
                    if key not in need or need[key][1] < d.ticket:
                        need[key] = (d.sem, d.ticket)
                for key, (sem, tk) in need.items():
                    eng.wait_ge(sem, tk)
                    waited[key] = tk
                ins = o.emit(eng)
                if o.is_dma:
                    ins.then_inc(o.sem, 16)
                elif o.need_signal:
                    ins.then_inc(o.sem, 1)
            if e == "sp":
                for d in final_wait_ops:
                    if waited.get(id(d.sem), 0) >= d.ticket:
                        continue
                    eng.wait_ge(d.sem, d.ticket)
                    waited[id(d.sem)] = d.ticket

        with nc.Block() as block:
            @block.tensor
            def _(eng):
                run("pe", eng)

            @block.scalar
            def _(eng):
                run("act", eng)

            @block.vector
            def _(eng):
                run("dve", eng)

            @block.gpsimd
            def _(eng):
                run("pool", eng)

            @block.sync
            def _(eng):
                run("sp", eng)


def build_program():
    nc = bass.Bass("TRN2", target_bir_lowering=False)

    def din(name, shape, dt=F32):
        return nc.dram_tensor(name, list(shape), dt, kind="ExternalInput").ap()

    xq = din("xq", [NG * 512, D])
    xc = din("xc", [S_LEN, D])
    memx = din("memx", [MEM, D])
    w_in = din("w_in", [D, INW])
    w_mkv = din("w_mkv", [D, 2 * 512])
    w_a = din("w_a", [768, D])
    w_b = din("w_b", [768, D])
    w_c = din("w_c", [512, D])
    w_o = din("w_o", [D, D])
    gnt_d = din("gnt", [128, 8])
    gmt_d = din("gmt", [128, 8])
    fing_d = din("fing", [1, D])
    lng_d = din("lng", [1, 768])
    wsp_d = din("wsp", [6, 128, 128])
    bsp_d = din("bsp", [6, 1, 128])
    ident_d = din("ident", [128, 128])
    tril_d = din("tril", [128, 128])
    rm_d = din("rm", [32, 32])
    ohot_d = din("ohot", [32, 32 * 128])
    cosc_d = din("cosc", [32, S_LEN])
    sinc_d = din("sinc", [32, S_LEN])
    cosq_d = din("cosq", [32, NG * 512])
    sinq_d = din("sinq", [32, NG * 512])
    cmask_d = din("cmask", [128, 8 * 512])
    vneg_d = din("vneg", [128, 16 * 32])
    vmask_d = din("vmask", [128, 16 * 32])
    obias_d = din("obias", [128, 16 * 32])
    out_d = nc.dram_tensor("y_out", [NG * 512, D], F32, kind="ExternalOutput").ap()
    kscr = nc.dram_tensor("kscr", [H, 128, S_LEN], BF16, kind="Internal").ap()
    vscr = nc.dram_tensor("vscr", [S_LEN, 768], BF16, kind="Internal").ap()

    S = Sched(nc)
    es = contextlib.ExitStack()

    def SB(name, shape, dt):
        return es.enter_context(nc.sbuf_tensor("s_" + name, list(shape), dt))

    def PS(name, shape, dt):
        return es.enter_context(nc.psum_tensor("p_" + name, list(shape), dt))

    with es:
        identb = SB("identb", [128, 128], BF16); b_identb = Buf("identb")
        trilf = SB("trilf", [128, 128], F32); b_tril = Buf("tril")
        rmb = SB("rmb", [32, 32], BF16); b_rm = Buf("rm")
        ohot = SB("ohot", [32, 32, 128], BF16); b_ohot = Buf("ohot")
        gnt = SB("gnt", [128, 8], F32); b_gnt = Buf("gnt")
        gmt = SB("gmt", [128, 8], F32); b_gmt = Buf("gmt")
        fing = SB("fing", [128, D], F32); b_fing = Buf("fing")
        lng = SB("lng", [128, 768], F32); b_lng = Buf("lng")
        bspc = [SB("bspc%d" % c, [128, 128], F32) for c in range(6)]
        b_bspc = [Buf("bspc%d" % c) for c in range(6)]
        wspf = SB("wspf", [128, 128], F32); b_wspf = Buf("wspf")
        wspm = SB("wspm", [128, 128], BF16); b_wspm = Buf("wspm")
        wspT = SB("wspT", [128, 6, 128], BF16); b_wspT = Buf("wspT")
        cmask = SB("cmask", [128, 8, 512], BF16); b_cmask = Buf("cmask")
        vneg = SB("vneg", [128, 16, 32], F32); b_vneg = Buf("vneg")
        vmask = SB("vmask", [128, 16, 32], F32); b_vmask = Buf("vmask")
        obias = SB("obias", [128, 16, 32], F32); b_obias = Buf("obias")
        kmf = SB("kmf", [128, H, NBLK], F32); b_kmf = Buf("kmf")
        kmb = SB("kmb", [128, H, NBLK], BF16); b_kmb = Buf("kmb")
        mkT = SB("mkT", [128, HC, MEM], BF16); b_mkT = Buf("mkT")
        mvt = SB("mvt", [128, 2, HC, 130], BF16); b_mvt = Buf("mvt")
        NXT = 3
        xt = [SB("xt%d" % i, [128, D], F32) for i in range(NXT)]
        b_xt = [Buf("xt%d" % i) for i in range(NXT)]
        junk = SB("junk", [128, D], BF16); b_junk = Buf("junk")
        NST = 4
        st_t = SB("st", [128, NST, 8], F32); b_st_l = [Buf("st%d" % i) for i in range(NST)]
        xn = [SB("xn%d" % i, [128, D], BF16) for i in range(2)]
        b_xn = [Buf("xn%d" % i) for i in range(2)]
        hT = SB("hT", [128, 8, 512], BF16); b_hTt = [Buf("hT%d" % i) for i in range(4)]
        NW = 3
        wb = [SB("wb%d" % i, [128, WBUF], BF16) for i in range(NW)]
        b_wb = [Buf("wb%d" % i) for i in range(NW)]
        cost = [SB("cost%d" % i, [32, 512], F32) for i in range(2)]
        sint = [SB("sint%d" % i, [32, 512], F32) for i in range(2)]
        b_cos = [Buf("cos%d" % i) for i in range(2)]
        b_sin = [Buf("sin%d" % i) for i in range(2)]
        t1 = [SB("t1_%d" % i, [32, 512], F32) for i in range(2)]
        b_t1 = [Buf("t1_%d" % i) for i in range(2)]
        NPT = 4
        pT = [SB("pT%d" % i, [128, 512], BF16) for i in range(NPT)]
        b_pT = [Buf("pT%d" % i) for i in range(NPT)]
        vn = SB("vn", [128, 4, 768], BF16); b_vn = [Buf("vn%d" % i) for i in range(4)]
        vtmp = SB("vtmp", [128, 768], F32); b_vtmp = Buf("vtmp")
        bnst = SB("bnst", [128, 3, 6], F32); b_bnst = Buf("bnst")
        qT = SB("qT", [128, H, 512], BF16); b_qT = [Buf("qT%d" % h) for h in range(H)]
        sga = SB("sga", [128, H, 512], BF16); b_sga = [Buf("sga%d" % h) for h in range(H)]
        oaT = SB("oaT", [128, H, 512], BF16); b_oaT = [Buf("oaT%d" % h) for h in range(H)]
        g1 = SB("g1", [128, H, 512], BF16); b_g1 = [Buf("g1_%d" % h) for h in range(H)]
        g2 = SB("g2", [128, H, 512], BF16); b_g2 = [Buf("g2_%d" % h) for h in range(H)]
        g3 = SB("g3", [128, H, 512], BF16); b_g3 = [Buf("g3_%d" % h) for h in range(H)]
        selbT = SB("selbT", [32, H, 512], BF16); b_selbT = [Buf("selbT%d" % h) for h in range(H)]
        bsm = SB("bsm", [128, H, 32], F32); b_bsm = Buf("bsm")
        top8 = SB("top8", [128, H, 8], F32); b_top8 = Buf("top8")
        selv = SB("selv", [128, H, 32], F32); b_selv = Buf("selv")
        biasb = SB("biasb", [128, 4, H, 32], BF16); b_biasb = Buf("biasb")
        NKC = 2
        kch = [SB("kch%d" % i, [128, 2048], BF16) for i in range(NKC)]
        vch = [SB("vch%d" % i, [128, 16, 130], BF16) for i in range(NKC)]
        b_kch = [Buf("kch%d" % i) for i in range(NKC)]
        b_vch = [Buf("vch%d" % i) for i in range(NKC)]
        rden = SB("rden", [128, 4], F32); b_rden = Buf("rden")
        oa = SB("oa", [128, 4, 128], BF16); b_oa = Buf("oa")
        tmpb = SB("tmpb", [128, 512], F32); b_tmpb = Buf("tmpb")
        usg = SB("usg", [128, 512], F32); b_usg = Buf("usg")
        smg = SB("smg", [128, 4, 512], BF16); b_smg = [Buf("smg%d" % i) for i in range(4)]
        yacc = SB("yacc", [128, 8, 512], BF16); b_yacc = [Buf("yacc%d" % i) for i in range(8)]
        yT = yacc; b_yT = b_yacc
        ytmp = SB("ytmp", [128, 512], F32); b_ytmp = Buf("ytmp")
        pp = [PS("pp%d" % i, [128, 512], F32) for i in range(2)]
        b_pp = [Buf("pp%d" % i, True) for i in range(2)]
        ptr = PS("ptr", [128, 1024], BF16)
        _bp = Buf("ptr", True); b_ptr = [_bp, _bp]
        pmisc = PS("pmisc", [128, 512], F32); b_pmisc = Buf("pmisc", True)
        psc = [PS("psc%d" % i, [128, 512], F32) for i in range(2)]
        b_psc = [Buf("psc%d" % i, True) for i in range(2)]
        pp = pp + psc
        b_pp = b_pp + b_psc
        psc = pp
        b_psc = b_pp
        NB4 = 4
        pacc_t = [PS("pacc%d" % i, [128, 512], F32) for i in range(2)]
        pacc = [t[:, 0:260].rearrange("p (a b) -> p a b", b=130) for t in pacc_t]
        b_pacc = [Buf("pacc%d" % i, True) for i in range(2)]
        b_kscr = [[Buf("kscr%d_%d" % (h, cg)) for cg in range(NCG)] for h in range(H)]
        b_vscr = [[Buf("vscr%d_%d" % (cg, tt)) for tt in range(4)] for cg in range(NCG)]
        b_out = Buf("out")

        ctr = dict(pp=0, w=0, xt=0, xn=0, cs=0, t=0, pT=0, kc=0, psc=0, res=0, st=0)

        def rot(key, n):
            v = ctr[key] % n
            ctr[key] += 1
            return v

        def ld(queue, dst, src, bdst):
            S.dma(queue, lambda e: e.dma_start(out=dst, in_=src), writes=[bdst])

        ld("pool", identb[:], ident_d[:, :], b_identb)
        ld("sp", trilf[:], tril_d[:, :], b_tril)
        ld("pool", rmb[:], rm_d[:, :], b_rm)
        ld("pool", ohot[:], ohot_d.rearrange("k (j m) -> k j m", m=128), b_ohot)
        ld("sp", gnt[:], gnt_d[:, :], b_gnt)
        ld("sp", gmt[:], gmt_d[:, :], b_gmt)
        ld("sp", fing[:], fing_d.partition_broadcast(128), b_fing)
        ld("sp", lng[:], lng_d.partition_broadcast(128), b_lng)
        for c in range(6):
            ld("sp", bspc[c][:], bsp_d[c].partition_broadcast(128), b_bspc[c])
        ld("pool", cmask[:], cmask_d.rearrange("p (a b) -> p a b", b=512), b_cmask)
        ld("sp", vneg[:], vneg_d.rearrange("p (a b) -> p a b", b=32), b_vneg)
        ld("sp", vmask[:], vmask_d.rearrange("p (a b) -> p a b", b=32), b_vmask)
        ld("sp", obias[:], obias_d.rearrange("p (a b) -> p a b", b=32), b_obias)
        S.op("dve", lambda e: e.memset(kmf[:], 0.0), writes=[b_kmf])
        S.op("dve", lambda e: e.memset(mvt[:, :, :, 128:130], 1.0), writes=[b_mvt])
        for i in range(NKC):
            S.op("dve", (lambda i: lambda e: e.memset(vch[i][:, :, 128:130], 1.0))(i), writes=[b_vch[i]])
        for c in range(6):
            ld("sp", wspf[:], wsp_d[c], b_wspf)
            S.op("dve", lambda e: e.tensor_tensor(out=wspm[:], in0=wspf[:], in1=trilf[:], op=ALU.mult),
                 reads=[b_wspf, b_tril], writes=[b_wspm])
            S.op("pe", lambda e: e.transpose(out=ptr[:, 0:128], in_=wspm[:], identity=identb[:]),
                 reads=[b_wspm, b_identb], writes=[b_ptr[0]])
            S.op("dve", (lambda c: lambda e: e.tensor_copy(out=wspT[:, c, :], in_=ptr[:, 0:128]))(c),
                 reads=[b_ptr[0]], writes=[b_wspT])

        wscr = {}

        def precast(name, src_ap, kc, c0, ncols):
            t = nc.dram_tensor("wscr_" + name, [128, kc * ncols], BF16, kind="Internal").ap()
            b = Buf("wscr_" + name)
            src = src_ap.rearrange("(k p) c -> p k c", p=128)[:, :, c0:c0 + ncols]
            dstv = t.rearrange("p (k c) -> p k c", c=ncols)
            S.dma("pool", lambda e: e.dma_start(out=dstv, in_=src), writes=[b])
            wscr[name] = (t, b)

        def load_w(src_ap, kc, c0, ncols, name=None):
            i = rot("w", NW)
            view = wb[i][:, 0:kc * ncols].rearrange("p (k c) -> p k c", c=ncols)
            if name is not None and name in wscr:
                t, b = wscr[name]
                S.dma("pool", lambda e: e.dma_start(out=wb[i][:, 0:kc * ncols], in_=t[:, :]), reads=[b], writes=[b_wb[i]])
            else:
                src = src_ap.rearrange("(k p) c -> p k c", p=128)[:, :, c0:c0 + ncols]
                S.dma("pool", lambda e: e.dma_start(out=view, in_=src), writes=[b_wb[i]])
            return view, b_wb[i]

        def load_x(src_ap):
            xi = rot("xt", NXT)
            S.dma("sp", lambda e: e.dma_start(out=xt[xi][:], in_=src_ap), writes=[b_xt[xi]])
            return xi

        def norm_tile(xi, tt, gvec, b_gvec):
            si = rot("st", NST)
            st = st_t[:, si, :]
            b_st = b_st_l[si]
            S.op("act", lambda e: e.activation(out=junk[:], in_=xt[xi][:], func=AF.Square, accum_out=st[:, 0:1]),
                 reads=[b_xt[xi]], writes=[b_junk, b_st])
            S.op("act", lambda e: e.activation(out=st[:, 2:3], in_=st[:, 0:1], func=AF.Sqrt, scale=1.0 / D, bias=EPS),
                 reads=[b_st], writes=[b_st])
            S.op("dve", lambda e: e.reciprocal(out=st[:, 3:4], in_=st[:, 2:3]), reads=[b_st], writes=[b_st])
            ni = rot("xn", 2)
            S.op("dve", lambda e: e.tensor_scalar(out=xn[ni][:], in0=xt[xi][:], scalar1=st[:, 3:4], scalar2=None, op0=ALU.mult),
                 reads=[b_xt[xi], b_st], writes=[b_xn[ni]])

            def tr(e):
                ins = None
                for k in range(8):
                    ins = e.transpose(out=ptr[:, k * 128:(k + 1) * 128], in_=xn[ni][:, k * 128:(k + 1) * 128],
                                      identity=identb[:])
                return ins
            S.op("pe", tr, reads=[b_xn[ni], b_identb], writes=[b_ptr[0], b_ptr[1]])
            S.op("dve", lambda e: e.tensor_tensor(
                out=hT[:, :, tt * 128:(tt + 1) * 128],
                in0=ptr[:, :].rearrange("p (k t) -> p k t", t=128),
                in1=gvec[:, :].unsqueeze(2).broadcast_to([128, 8, 128]), op=ALU.mult),
                reads=[b_ptr[0], b_ptr[1], b_gvec], writes=[b_hTt[tt]])

        def make_hT(src_rows, ntiles, gvec, b_gvec):
            for tt in range(ntiles):
                norm_tile(load_x(src_rows(tt)), tt, gvec, b_gvec)

        def proj_fm(wview, b_w, c0, ntok=512, kcn=8, rhs_fn=None, rhs_bufs=None):
            bi = rot("pp", NB4)
            if rhs_fn is None:
                rhs_fn = lambda k: hT[:, k, 0:ntok]
                rhs_bufs = b_hTt

            def f(e):
                ins = None
                for k in range(kcn):
                    ins = e.matmul(pp[bi][:, 0:ntok], lhsT=wview[:, k, c0:c0 + 128], rhs=rhs_fn(k),
                                   start=(k == 0), stop=(k == kcn - 1))
                return ins
            S.op("pe", f, reads=[b_w] + list(rhs_bufs), writes=[b_pp[bi]])
            return bi

        def rope_evac(bi, dst, b_dst, csi, pending):
            S.op("act", lambda e: e.activation(out=dst, in_=pp[bi][:], func=AF.Copy), reads=[b_pp[bi]], writes=[b_dst])
            ti = rot("t", 2)
            if _DBG['t1'] == 1:
                S.op("dve", lambda e: e.tensor_tensor(out=t1[ti][:], in0=pp[bi][0:32, :], in1=cost[csi][:], op=ALU.mult),
                     reads=[b_pp[bi], b_cos[csi]], writes=[b_t1[ti]])
            elif _DBG['t1'] == 2:
                S.op("dve", lambda e: e.tensor_tensor(out=t1[ti][:], in0=fing[0:32, 0:512], in1=cost[csi][:], op=ALU.mult),
                     reads=[b_fing, b_cos[csi]], writes=[b_t1[ti]])
            elif _DBG['t1'] == 3:
                S.op("dve", lambda e: e.tensor_tensor(out=t1[ti][:], in0=pp[bi][0:32, :], in1=fing[0:32, 0:512], op=ALU.mult),
                     reads=[b_pp[bi], b_fing], writes=[b_t1[ti]])

            def later():
                S.op("pe", lambda e: e.matmul(pmisc[0:32, :], lhsT=rmb[:, :], rhs=dst[0:32, :], start=True, stop=True),
                     reads=[b_rm, b_dst], writes=[b_pmisc])
                S.op("dve", lambda e: e.tensor_tensor(out=pmisc[0:32, :], in0=pmisc[0:32, :], in1=sint[csi][:], op=ALU.mult),
                     reads=[b_pmisc, b_sin[csi]], writes=[b_pmisc])
                S.op("dve", lambda e: e.tensor_tensor(out=dst[0:32, :], in0=pmisc[0:32, :], in1=t1[ti][:], op=ALU.add),
                     reads=[b_t1[ti], b_pmisc], writes=[b_dst])
            pending.append(later)

        def load_cs(cos_d, sin_d, col0):
            csi = rot("cs", 2)
            S.dma("sp", lambda e: e.dma_start(out=cost[csi][:], in_=cos_d[:, col0:col0 + 512]), writes=[b_cos[csi]])
            S.dma("sp", lambda e: e.dma_start(out=sint[csi][:], in_=sin_d[:, col0:col0 + 512]), writes=[b_sin[csi]])
            return csi

        def attn_core(ktiles, q_ap, b_q, post, loaders=None):
            n = len(ktiles)
            slots = {}

            def qk(i):
                kt = ktiles[i]
                bi = rot("psc", NB4)
                slots[i] = bi
                ex = kt["extra"]

                def f(e):
                    ins = e.matmul(psc[bi][:], lhsT=kt["k"], rhs=q_ap, start=True, stop=(len(ex) == 0))
                    for j, (l, r, _) in enumerate(ex):
                        ins = e.matmul(psc[bi][:], lhsT=l, rhs=r, start=False, stop=(j == len(ex) - 1))
                    return ins
                rd = list(kt["kb"]) + [b_q]
                for (_, _, bufs) in ex:
                    rd += list(bufs)
                S.op("pe", f, reads=rd, writes=[b_psc[bi]])

            def ex_pv(i):
                kt = ktiles[i]
                bi = slots[i]
                pi = rot("pT", NPT)
                S.op("act", lambda e: e.activation(out=pT[pi][:], in_=psc[bi][:], func=AF.Exp, scale=SCALE),
                     reads=[b_psc[bi]], writes=[b_pT[pi]])
                return pi

            def pv(i, pi):
                kt = ktiles[i]

                def f(e):
                    ins = None
                    for qt in range(4):
                        ins = e.matmul(pacc[qt // 2][:, qt % 2, 0:129], lhsT=pT[pi][:, qt * 128:(qt + 1) * 128],
                                       rhs=kt["v"], start=(i == 0 and qt % 2 == 0), stop=(i == n - 1),
                                       skip_group_check=True)
                    return ins
                S.op("pe", f, reads=[b_pT[pi]] + list(kt["vb"]), writes=[b_pacc[0], b_pacc[1]])

            loaded = set()

            def need(c):
                if loaders is not None and c is not None and c < len(loaders) and c not in loaded:
                    loaded.add(c)
                    loaders[c]()

            AH = 3

            def prologue():
                for j in range(min(AH, n)):
                    need(ktiles[j].get("chunk"))
                    qk(j)

            def body():
                for i in range(n):
                    pi = ex_pv(i)
                    if i + AH < n:
                        need(ktiles[i + AH].get("chunk"))
                        qk(i + AH)
                    pv(i, pi)
                    c = ktiles[i].get("chunk")
                    if c is not None and i % 16 == 1:
                        need(c + 1)
            return prologue, body, post

        def run_stages(stages):
            for i, (pro, body, post) in enumerate(stages):
                if i == 0:
                    pro()
                body()
                if i + 1 < len(stages):
                    stages[i + 1][0]()
                post()

        def attn_post(gate_ap, b_gate, dst_ap, b_dst):
            for bk in range(2):
                S.op("dve", (lambda bk: lambda e: e.reciprocal(out=rden[:, 2 * bk:2 * bk + 2].unsqueeze(2),
                                                              in_=pacc[bk][:, :, 128:129]))(bk),
                     reads=[b_pacc[bk]], writes=[b_rden])
            for qt in range(4):
                if qt < 2:
                    S.op("dve", (lambda qt: lambda e: e.tensor_scalar(out=oa[:, qt, :], in0=pacc[qt // 2][:, qt % 2, 0:128],
                                                                      scalar1=rden[:, qt:qt + 1], scalar2=None,
                                                                      op0=ALU.mult))(qt),
                         reads=[b_pacc[qt // 2], b_rden], writes=[b_oa])
                else:
                    S.op("act", (lambda qt: lambda e: e.activation(out=oa[:, qt, :], in_=pacc[qt // 2][:, qt % 2, 0:128],
                                                                   func=AF.Copy, scale=rden[:, qt:qt + 1]))(qt),
                         reads=[b_pacc[qt // 2], b_rden], writes=[b_oa])

            def tr(e):
                ins = None
                for qt in range(4):
                    ins = e.transpose(out=ptr[:, qt * 128:(qt + 1) * 128], in_=oa[:, qt, :], identity=identb[:])
                return ins
            S.op("pe", tr, reads=[b_oa, b_identb], writes=[b_ptr[0]])
            S.op("dve", lambda e: e.tensor_tensor(out=dst_ap, in0=ptr[:, 0:512], in1=gate_ap, op=ALU.mult),
                 reads=[b_ptr[0], b_gate], writes=[b_dst])

        make_hT(lambda tt: memx[tt * 128:(tt + 1) * 128, :], 2, gmt, b_gmt)
        wk_v, wk_b = load_w(w_mkv, 8, 0, 512)
        wv_v, wv_b = load_w(w_mkv, 8, 512, 512)
        for hc in range(HC):
            bi = proj_fm(wk_v, wk_b, hc * 128, ntok=MEM)
            S.op("act", (lambda bi, hc: lambda e: e.activation(out=mkT[:, hc, :], in_=pp[bi][:, 0:MEM], func=AF.Copy))(bi, hc),
                 reads=[b_pp[bi]], writes=[b_mkT])
        for mt in range(2):
            bi = rot("pp", 2)

            def f(e, bi=bi, mt=mt, wv_v=wv_v):
                ins = None
                for k in range(8):
                    ins = e.matmul(pp[bi][:, :], lhsT=hT[:, k, mt * 128:(mt + 1) * 128], rhs=wv_v[:, k, :],
                                   start=(k == 0), stop=(k == 7))
                return ins
            S.op("pe", f, reads=b_hTt + [wv_b], writes=[b_pp[bi]])
            S.op("act", (lambda bi, mt: lambda e: e.activation(
                out=mvt[:, mt, :, 0:128], in_=pp[bi][:, :].rearrange("p (h d) -> p h d", d=128), func=AF.Copy))(bi, mt),
                reads=[b_pp[bi]], writes=[b_mvt])

        wk_v, wk_b = load_w(w_in, 8, OFF["ka"], 768)
        wv_v, wv_b = load_w(w_in, 8, OFF["va"], 768)
        for nm in ("qa", "ga", "ub", "gb", "vb"):
            precast(nm, w_in, 8, OFF[nm], 768)
        precast("m%d_0" % OFF["mb"], w_in, 8, OFF["mb"], 512)
        precast("w_b", w_b, 6, 0, D)
        precast("m%d_1" % OFF["mb"], w_in, 8, OFF["mb"] + 512, 512)
        for nm in ("qc", "gc"):
            precast(nm, w_in, 8, OFF[nm], 512)
        precast("m%d_0" % OFF["mc"], w_in, 8, OFF["mc"], 512)
        precast("w_c", w_c, 4, 0, D)
        precast("m%d_1" % OFF["mc"], w_in, 8, OFF["mc"] + 512, 512)
        precast("m%d_0" % OFF["ma"], w_in, 8, OFF["ma"], 512)
        precast("w_a", w_a, 6, 0, D)
        precast("m%d_1" % OFF["ma"], w_in, 8, OFF["ma"] + 512, 512)
        precast("wo0", w_o, 8, 0, 512)
        precast("wo1", w_o, 8, 512, 512)
        def xc_rows(cg, tt):
            return xc[cg * 512 + tt * 128: cg * 512 + (tt + 1) * 128, :]

        if _DBG['ncg'] > 0:
            make_hT(lambda tt: xc_rows(0, tt), 4, gnt, b_gnt)
        for cg in range(_DBG['ncg']):
            nxt = cg + 1 < _DBG['ncg']
            pre_x = [load_x(xc_rows(cg + 1, tt)) for tt in range(3)] if nxt else []
            csi = load_cs(cosc_d, sinc_d, cg * 512)
            prev = None
            for h in range(H if _DBG['k'] else 0):
                bi = proj_fm(wk_v, wk_b, h * 128)
                pi = rot("pT", NPT)
                cur = []
                rope_evac(bi, pT[pi][:, :], b_pT[pi], csi, cur)

                def fin(pi=pi, h=h, cg=cg):
                    S.op("dve", lambda e: e.tensor_reduce(out=kmf[:, h, 2 * cg:2 * cg + 2],
                                                          in_=pT[pi][:, :].rearrange("p (a b) -> p a b", b=256),
                                                          axis=AX.X, op=ALU.add),
                         reads=[b_pT[pi]], writes=[b_kmf])
                    S.dma("sp", lambda e: e.dma_start(out=kscr[h, :, cg * 512:(cg + 1) * 512], in_=pT[pi][:, :]),
                          reads=[b_pT[pi]], writes=[b_kscr[h][cg]], sem_buf=b_pT[pi])
                if not _DBG['later']:
                    cur = []
                if _DBG['fin']:
                    cur.append(fin)
                if prev is not None:
                    for fn in prev:
                        fn()
                prev = cur
            for fn in (prev or []):
                fn()
            for tt in range(4 if _DBG['v'] else 0):
                p0, p1 = 2 * (tt % 2), 2 * (tt % 2) + 1

                def f(e, tt=tt, wv_v=wv_v, p0=p0, p1=p1):
                    ins = None
                    for k in range(8):
                        ins = e.matmul(pp[p0][:, :], lhsT=hT[:, k, tt * 128:(tt + 1) * 128], rhs=wv_v[:, k, 0:512],
                                       start=(k == 0), stop=(k == 7))
                    for k in range(8):
                        ins = e.matmul(pp[p1][:, 0:256], lhsT=hT[:, k, tt * 128:(tt + 1) * 128], rhs=wv_v[:, k, 512:768],
                                       start=(k == 0), stop=(k == 7))
                    return ins
                S.op("pe", f, reads=[b_hTt[tt], wv_b], writes=[b_pp[p0], b_pp[p1]])
                S.op("act", (lambda tt, p0: lambda e: e.activation(out=vn[:, tt, 0:512], in_=pp[p0][:, :], func=AF.Copy))(tt, p0),
                     reads=[b_pp[p0]], writes=[b_vn[tt]])
                S.op("act", (lambda tt, p1: lambda e: e.activation(out=vn[:, tt, 512:768], in_=pp[p1][:, 0:256], func=AF.Copy))(tt, p1),
                     reads=[b_pp[p1]], writes=[b_vn[tt]])
                S.dma("sp", (lambda tt, cg: lambda e: e.dma_start(
                    out=vscr[cg * 512 + tt * 128: cg * 512 + (tt + 1) * 128, :], in_=vn[:, tt, :]))(tt, cg),
                    reads=[b_vn[tt]], writes=[b_vscr[cg][tt]], sem_buf=b_vn[tt])
                if nxt:
                    if tt == 0:
                        norm_tile(pre_x[0], 0, gnt, b_gnt)
                        pre_x.append(load_x(xc_rows(cg + 1, 3)))
                    else:
                        norm_tile(pre_x[tt], tt, gnt, b_gnt)
        S.op("dve", lambda e: e.tensor_scalar(out=kmb[:], in0=kmf[:], scalar1=1.0 / 256, scalar2=None, op0=ALU.mult),
             reads=[b_kmf], writes=[b_kmb])

        out_dmas = []
        for g in _DBG.get('glist', range(_DBG['ng'])):
            make_hT(lambda tt, g=g: xq[g * 512 + tt * 128: g * 512 + (tt + 1) * 128, :], 4, gnt, b_gnt)
            csi = load_cs(cosq_d, sinq_d, g * 512)
            wv_, wb_ = load_w(w_in, 8, OFF["qa"], 768, "qa")
            prev = None
            for h in range(H):
                bi = proj_fm(wv_, wb_, h * 128)
                cur = []
                rope_evac(bi, qT[:, h, :], b_qT[h], csi, cur)
                if prev is not None:
                    for fn in prev:
                        fn()
                prev = cur
            for fn in prev:
                fn()
            fillers = []
            w_ga = load_w(w_in, 8, OFF["ga"], 768, "ga")
            w_ub = load_w(w_in, 8, OFF["ub"], 768, "ub")
            w_gb = load_w(w_in, 8, OFF["gb"], 768, "gb")

            def mk_filler(wv_wb, c, dst, b_dst, func):
                def fn():
                    bi = proj_fm(wv_wb[0], wv_wb[1], c * 128)
                    S.op("act", lambda e: e.activation(out=dst, in_=pp[bi][:], func=func), reads=[b_pp[bi]], writes=[b_dst])
                return fn
            for c in range(6):
                fillers.append(mk_filler(w_ga, c, sga[:, c, :], b_sga[c], AF.Silu))
            for c in range(6):
                fillers.append(mk_filler(w_ub, c, g1[:, c, :], b_g1[c], AF.Copy))
            for c in range(6):
                fillers.append(mk_filler(w_gb, c, g2[:, c, :], b_g2[c], AF.Silu))
            for qt in range(4):
                gs = 2 * g + qt // 2

                def f(e, qt=qt):
                    ins = None
                    for h in range(H):
                        ins = e.matmul(pmisc[:, h * 32:(h + 1) * 32], lhsT=qT[:, h, qt * 128:(qt + 1) * 128],
                                       rhs=kmb[:, h, :], start=True, stop=True)
                    return ins
                S.op("pe", f, reads=b_qT + [b_kmb], writes=[b_pmisc])
                for h in range(H):
                    S.op("dve", (lambda h, gs: lambda e: e.tensor_tensor(out=bsm[:, h, :], in0=pmisc[:, h * 32:(h + 1) * 32],
                                                                         in1=vneg[:, gs, :], op=ALU.add))(h, gs),
                         reads=[b_pmisc, b_vneg], writes=[b_bsm])
                for h in range(H):
                    S.op("dve", (lambda h: lambda e: e.max(out=top8[:, h, :], in_=bsm[:, h, :]))(h),
                         reads=[b_bsm], writes=[b_top8])
                for h in range(H):
                    S.op("dve", (lambda h, gs: lambda e: e.scalar_tensor_tensor(
                        out=selv[:, h, :], in0=bsm[:, h, :], scalar=top8[:, h, 2:3], in1=vmask[:, gs, :],
                        op0=ALU.is_ge, op1=ALU.mult))(h, gs),
                        reads=[b_bsm, b_top8, b_vmask], writes=[b_selv])
                for h in range(H):
                    S.op("dve", (lambda h, gs, qt: lambda e: e.scalar_tensor_tensor(
                        out=biasb[:, qt, h, :], in0=selv[:, h, :], scalar=-NEG, in1=obias[:, gs, :],
                        op0=ALU.mult, op1=ALU.add))(h, gs, qt),
                        reads=[b_selv, b_obias], writes=[b_biasb])
                for _ in range(4):
                    if fillers:
                        fillers.pop(0)()
            while fillers:
                fillers.pop(0)()
            for h in range(H):
                half = h % 2

                def tr(e, h=h, half=half):
                    ins = None
                    for qt in range(4):
                        ins = e.transpose(out=ptr[0:32, half * 512 + qt * 128: half * 512 + (qt + 1) * 128],
                                          in_=biasb[:, qt, h, :], identity=identb[:])
                    return ins
                S.op("pe", tr, reads=[b_biasb, b_identb], writes=[b_ptr[half]])
                S.op("act", (lambda h, half: lambda e: e.activation(out=selbT[:, h, :],
                                                                    in_=ptr[0:32, half * 512:(half + 1) * 512],
                                                                    func=AF.Copy))(h, half),
                     reads=[b_ptr[half]], writes=[b_selbT[h]])
            nkt = 8 * (g + 1)
            stages = []
            for h in range(H):
                ktiles = []
                loaders = []
                for c0 in range(0, nkt, 16):
                    nt = min(16, nkt - c0)
                    ki = rot("kc", NKC)

                    def loader(ki=ki, h=h, c0=c0, nt=nt):
                        S.dma("sp", lambda e: e.dma_start(
                            out=kch[ki][:, 0:nt * 128], in_=kscr[h, :, c0 * 128:(c0 + nt) * 128]),
                            reads=[b_kscr[h][cg] for cg in range(c0 // 4, (c0 + nt + 3) // 4)], writes=[b_kch[ki]])
                        S.dma("sp", lambda e: e.dma_start(
                            out=vch[ki][:, 0:nt, 0:128],
                            in_=vscr.rearrange("(t p) c -> p t c", p=128)[:, c0:c0 + nt, h * 128:(h + 1) * 128]),
                            reads=[b_vscr[cg][t4] for cg in range(c0 // 4, (c0 + nt + 3) // 4) for t4 in range(4)],
                            writes=[b_vch[ki]])
                    loaders.append(loader)
                    for j in range(nt):
                        kt = c0 + j
                        extra = [(ohot[:, kt // 2, :], selbT[:, h, :], [b_ohot, b_selbT[h]])]
                        if kt >= nkt - 8:
                            extra.append((identb[:, :], cmask[:, kt - (nkt - 8), :], [b_identb, b_cmask]))
                        ktiles.append(dict(k=kch[ki][:, j * 128:(j + 1) * 128], kb=[b_kch[ki]],
                                           v=vch[ki][:, j, 0:129], vb=[b_vch[ki]], extra=extra, chunk=c0 // 16))
                stages.append(attn_core(ktiles, qT[:, h, :], b_qT[h],
                                        (lambda h: lambda: attn_post(sga[:, h, :], b_sga[h], oaT[:, h, :], b_oaT[h]))(h),
                                        loaders=loaders))
            run_stages(stages)

            wv_, wb_ = load_w(w_in, 8, OFF["vb"], 768, "vb")
            for tt in range(4):
                p0, p1 = 2 * (tt % 2), 2 * (tt % 2) + 1

                def f(e, tt=tt, wv_=wv_, p0=p0, p1=p1):
                    ins = None
                    for k in range(8):
                        ins = e.matmul(pp[p0][:, :], lhsT=hT[:, k, tt * 128:(tt + 1) * 128], rhs=wv_[:, k, 0:512],
                                       start=(k == 0), stop=(k == 7))
                    for k in range(8):
                        ins = e.matmul(pp[p1][:, 0:256], lhsT=hT[:, k, tt * 128:(tt + 1) * 128], rhs=wv_[:, k, 512:768],
                                       start=(k == 0), stop=(k == 7))
                    return ins
                S.op("pe", f, reads=[b_hTt[tt], wb_], writes=[b_pp[p0], b_pp[p1]])
                S.op("dve", lambda e, p0=p0, p1=p1: e.bn_stats(out=bnst[:, 0, :], in_=pp[p0][:, 0:256]), reads=[b_pp[p0]], writes=[b_bnst])
                S.op("dve", lambda e, p0=p0, p1=p1: e.bn_stats(out=bnst[:, 1, :], in_=pp[p0][:, 256:512]), reads=[b_pp[p0]], writes=[b_bnst])
                S.op("dve", lambda e, p0=p0, p1=p1: e.bn_stats(out=bnst[:, 2, :], in_=pp[p1][:, 0:256]), reads=[b_pp[p1]], writes=[b_bnst])
                si = rot("st", NST)
                st = st_t[:, si, :]
                b_st = b_st_l[si]
                S.op("dve", (lambda st: lambda e: e.bn_aggr(out=st[:, 4:6], in_=bnst[:, :, :].rearrange("p a b -> p (a b)")))(st),
                     reads=[b_bnst], writes=[b_st])
                S.op("act", (lambda st: lambda e: e.activation(out=st[:, 6:7], in_=st[:, 5:6], func=AF.Sqrt, scale=1.0, bias=EPS))(st),
                     reads=[b_st], writes=[b_st])
                S.op("dve", (lambda st: lambda e: e.reciprocal(out=st[:, 7:8], in_=st[:, 6:7]))(st), reads=[b_st], writes=[b_st])
                S.op("dve", (lambda st, p0=p0, p1=p1: lambda e: e.tensor_scalar(out=vtmp[:, 0:512], in0=pp[p0][:, :], scalar1=st[:, 4:5], scalar2=st[:, 7:8],
                                                                  op0=ALU.subtract, op1=ALU.mult))(st),
                     reads=[b_pp[p0], b_st], writes=[b_vtmp])
                S.op("dve", (lambda st, p0=p0, p1=p1: lambda e: e.tensor_scalar(out=vtmp[:, 512:768], in0=pp[p1][:, 0:256], scalar1=st[:, 4:5], scalar2=st[:, 7:8],
                                                                  op0=ALU.subtract, op1=ALU.mult))(st),
                     reads=[b_pp[p1], b_st], writes=[b_vtmp])
                S.op("pool", (lambda tt: lambda e: e.tensor_tensor(out=vn[:, tt, :], in0=vtmp[:, :], in1=lng[:, :], op=ALU.mult))(tt),
                     reads=[b_vtmp, b_lng], writes=[b_vn[tt]])
            for c in range(6):
                bi = rot("pp", 2)

                def f(e, bi=bi, c=c):
                    ins = None
                    for tt in range(4):
                        ins = e.matmul(pp[bi][:, tt * 128:(tt + 1) * 128], lhsT=vn[:, tt, c * 128:(c + 1) * 128],
                                       rhs=wspT[:, c, :], start=True, stop=True)
                    return ins
                S.op("pe", f, reads=b_vn + [b_wspT], writes=[b_pp[bi]])
                S.op("dve", (lambda bi, c: lambda e: e.tensor_tensor(
                    out=tmpb[:, :].rearrange("p (a b) -> p a b", b=128),
                    in0=pp[bi][:, :].rearrange("p (a b) -> p a b", b=128),
                    in1=bspc[c][:, :].unsqueeze(1).broadcast_to([128, 4, 128]), op=ALU.add))(bi, c),
                    reads=[b_pp[bi], b_bspc[c]], writes=[b_tmpb])
                S.op("pool", (lambda c: lambda e: e.tensor_tensor(out=usg[:, :], in0=g1[:, c, :], in1=g2[:, c, :], op=ALU.mult))(c),
                     reads=[b_g1[c], b_g2[c]], writes=[b_usg])
                S.op("dve", (lambda c: lambda e: e.tensor_tensor(out=g3[:, c, :], in0=tmpb[:, :], in1=usg[:, :], op=ALU.mult))(c),
                     reads=[b_tmpb, b_usg], writes=[b_g3[c]])

            def merge(first, last, moff, wsrc, kcn, srcT, b_src, wname):
                wv3, wb3 = load_w(wsrc, kcn, 0, D, wname)
                for half in range(2):
                    wv2, wb2 = load_w(w_in, 8, moff + half * 512, 512, "m%d_%d" % (moff, half))
                    for c in range(4):
                        bi = proj_fm(wv2, wb2, c * 128)
                        S.op("act", (lambda bi, c: lambda e: e.activation(out=smg[:, c, :], in_=pp[bi][:], func=AF.Sigmoid))(bi, c),
                             reads=[b_pp[bi]], writes=[b_smg[c]])
                    for c in range(4):
                        oc = half * 4 + c
                        bi = proj_fm(wv3, wb3, oc * 128, kcn=kcn, rhs_fn=lambda k: srcT[:, k, :], rhs_bufs=b_src[0:kcn])
                        if first:
                            S.op("dve", (lambda bi, oc, c: lambda e: e.tensor_tensor(out=yacc[:, oc, :], in0=pp[bi][:], in1=smg[:, c, :],
                                                                                     op=ALU.mult))(bi, oc, c),
                                 reads=[b_pp[bi], b_smg[c]], writes=[b_yacc[oc]])
                        else:
                            S.op("dve", (lambda bi, c: lambda e: e.tensor_tensor(out=ytmp[:, :], in0=pp[bi][:], in1=smg[:, c, :],
                                                                                 op=ALU.mult))(bi, c),
                                 reads=[b_pp[bi], b_smg[c]], writes=[b_ytmp])
                            S.op("dve", (lambda oc: lambda e: e.tensor_tensor(out=yacc[:, oc, :], in0=yacc[:, oc, :], in1=ytmp[:, :],
                                                                              op=ALU.add))(oc),
                                 reads=[b_yacc[oc], b_ytmp], writes=[b_yacc[oc]])

            merge(True, False, OFF["mb"], w_b, 6, g3, b_g3, "w_b")

            wv_, wb_ = load_w(w_in, 8, OFF["qc"], 512, "qc")
            for c in range(HC):
                bi = proj_fm(wv_, wb_, c * 128)
                S.op("act", (lambda bi, c: lambda e: e.activation(out=g1[:, c, :], in_=pp[bi][:], func=AF.Copy))(bi, c),
                     reads=[b_pp[bi]], writes=[b_g1[c]])
            wv_, wb_ = load_w(w_in, 8, OFF["gc"], 512, "gc")
            for c in range(HC):
                bi = proj_fm(wv_, wb_, c * 128)
                S.op("act", (lambda bi, c: lambda e: e.activation(out=g2[:, c, :], in_=pp[bi][:], func=AF.Silu))(bi, c),
                     reads=[b_pp[bi]], writes=[b_g2[c]])
            stages = []
            for hc in range(HC):
                ktiles = [dict(k=mkT[:, hc, mt * 128:(mt + 1) * 128], kb=[b_mkT], v=mvt[:, mt, hc, 0:129], vb=[b_mvt], extra=[])
                          for mt in range(2)]
                stages.append(attn_core(ktiles, g1[:, hc, :], b_g1[hc],
                                        (lambda hc: lambda: attn_post(g2[:, hc, :], b_g2[hc], g3[:, hc, :], b_g3[hc]))(hc)))
            run_stages(stages)
            merge(False, False, OFF["mc"], w_c, 4, g3, b_g3, "w_c")
            merge(False, True, OFF["ma"], w_a, 6, oaT, b_oaT, "w_a")

            wo0, wob0 = load_w(w_o, 8, 0, 512, "wo0")
            wo1, wob1 = load_w(w_o, 8, 512, 512, "wo1")
            for tt in range(4):
                xi = rot("xt", NXT)
                S.dma("sp", (lambda xi, tt, g: lambda e: e.dma_start(
                    out=xt[xi][:], in_=xq[g * 512 + tt * 128: g * 512 + (tt + 1) * 128, :]))(xi, tt, g),
                    writes=[b_xt[xi]])

                p0, p1 = 2 * (tt % 2), 2 * (tt % 2) + 1

                def f(e, tt=tt, wo0=wo0, wo1=wo1, p0=p0, p1=p1):
                    ins = None
                    for k in range(8):
                        ins = e.matmul(pp[p0][:, :], lhsT=yT[:, k, tt * 128:(tt + 1) * 128], rhs=wo0[:, k, :],
                                       start=(k == 0), stop=(k == 7))
                    for k in range(8):
                        ins = e.matmul(pp[p1][:, :], lhsT=yT[:, k, tt * 128:(tt + 1) * 128], rhs=wo1[:, k, :],
                                       start=(k == 0), stop=(k == 7))
                    return ins
                S.op("pe", f, reads=b_yT + [wob0, wob1], writes=[b_pp[p0], b_pp[p1]])
                ri = xi
                S.op("dve", (lambda ri, xi, p0=p0, p1=p1: lambda e: e.tensor_tensor(out=xt[ri][:, 0:512], in0=pp[p0][:, :], in1=xt[xi][:, 0:512],
                                                                      op=ALU.add))(ri, xi),
                     reads=[b_pp[p0], b_xt[xi]], writes=[b_xt[ri]])
                S.op("dve", (lambda ri, xi, p0=p0, p1=p1: lambda e: e.tensor_tensor(out=xt[ri][:, 512:1024], in0=pp[p1][:, :], in1=xt[xi][:, 512:1024],
                                                                      op=ALU.add))(ri, xi),
                     reads=[b_pp[p1], b_xt[xi]], writes=[b_xt[ri]])
                si = rot("st", NST)
                st = st_t[:, si, :]
                b_st = b_st_l[si]
                S.op("act", (lambda ri, st: lambda e: e.activation(out=junk[:], in_=xt[ri][:], func=AF.Square, accum_out=st[:, 0:1]))(ri, st),
                     reads=[b_xt[ri]], writes=[b_junk, b_st])
                S.op("act", (lambda st: lambda e: e.activation(out=st[:, 2:3], in_=st[:, 0:1], func=AF.Sqrt, scale=1.0 / D, bias=EPS))(st),
                     reads=[b_st], writes=[b_st])
                S.op("dve", (lambda st: lambda e: e.reciprocal(out=st[:, 3:4], in_=st[:, 2:3]))(st), reads=[b_st], writes=[b_st])
                S.op("dve", (lambda ri, st: lambda e: e.scalar_tensor_tensor(out=xt[ri][:], in0=xt[ri][:], scalar=st[:, 3:4], in1=fing[:],
                                                                             op0=ALU.mult, op1=ALU.mult))(ri, st),
                     reads=[b_xt[ri], b_st, b_fing], writes=[b_xt[ri]])
                od = S.dma("sp", (lambda ri, tt, g: lambda e: e.dma_start(
                    out=out_d[g * 512 + tt * 128: g * 512 + (tt + 1) * 128, :], in_=xt[ri][:]))(ri, tt, g),
                    reads=[b_xt[ri]], writes=[Buf("o")], sem_buf=b_xt[ri])
                out_dmas.append(od)

        S.finalize(out_dmas[-2:])
    return nc


def _own_blocks(r):
    blocks = []
    for g in range(NG):
        blocks.append(4 * g + (0 if r == 0 else 1))
        blocks.append(4 * g + (3 if r == 0 else 2))
    return blocks


def _host_tables(r):
    blocks = _own_blocks(r)
    half = 8
    inv_freq = (500000.0 ** (-np.arange(0, 32, 2, dtype=np.float32) / 32)).astype(np.float32)
    qpos = np.concatenate([np.arange(b * 256, (b + 1) * 256) for b in blocks])

    def cs(pos):
        ang = pos.astype(np.float32)[:, None] * inv_freq[None, :]
        c = np.cos(ang).astype(np.float32).T
        s = np.sin(ang).astype(np.float32).T
        return np.ascontiguousarray(np.concatenate([c, c], 0)), np.ascontiguousarray(np.concatenate([s, s], 0))
    cosq, sinq = cs(qpos)
    cosc, sinc = cs(np.arange(S_LEN))
    cm = np.zeros((128, 8, 512), np.float32)
    for kt in range(8):
        kpos = kt * 128 + np.arange(128)
        for slot in range(2):
            rel = blocks[slot] * 256 + np.arange(256)
            cm[:, kt, slot * 256:(slot + 1) * 256] = np.where(kpos[:, None] <= rel[None, :], 0.0, NEG)
    vneg = np.zeros((128, 16, 32), np.float32)
    vmask = np.zeros((128, 16, 32), np.float32)
    obias = np.full((128, 16, 32), NEG, np.float32)
    for i, b in enumerate(blocks):
        vmask[:, i, :b] = 1.0
        vneg[:, i, b:] = -1e30
        obias[:, i, b] = 0.0
    return dict(cosq=cosq, sinq=sinq, cosc=cosc, sinc=sinc, cmask=cm.reshape(128, -1),
                vneg=vneg.reshape(128, -1), vmask=vmask.reshape(128, -1), obias=obias.reshape(128, -1))


_NC_CACHE = {}


def kernel(x, mem, norm_g, mem_norm_g, final_norm_g, w_in, w_mem_kv, gmlp_ln_g, w_spatial, b_spatial,
           w_branch_a, w_branch_b, w_branch_c, w_out):
    f = lambda a: np.ascontiguousarray(np.asarray(a, dtype=np.float32))
    x = f(x); mem = f(mem)
    if "nc" not in _NC_CACHE:
        _NC_CACHE["nc"] = build_program()
    nc = _NC_CACHE["nc"]
    rm = np.zeros((32, 32), np.float32)
    for i in range(16):
        rm[i + 16, i] = -1.0
        rm[i, i + 16] = 1.0
    ohot = np.zeros((32, 32, 128), np.float32)
    for j in range(32):
        ohot[j, j, :] = 1.0
    common = dict(
        w_in=f(w_in[0]), w_mkv=f(w_mem_kv[0]), w_a=f(w_branch_a[0]), w_b=f(w_branch_b[0]), w_c=f(w_branch_c[0]),
        w_o=f(w_out[0]),
        gnt=f(np.asarray(norm_g[0]).reshape(8, 128).T), gmt=f(np.asarray(mem_norm_g[0]).reshape(8, 128).T),
        fing=f(np.asarray(final_norm_g).reshape(1, D)), lng=f(np.asarray(gmlp_ln_g[0]).reshape(1, 768)),
        wsp=f(w_spatial[0]), bsp=f(np.asarray(b_spatial[0]).reshape(6, 1, 128)),
        ident=np.eye(128, dtype=np.float32), tril=np.tril(np.ones((128, 128), np.float32)),
        rm=rm, ohot=ohot.reshape(32, -1),
    )
    tabs = [_host_tables(0), _host_tables(1)]
    in_maps = []
    rows = []
    for c in range(8):
        b, r = c // 2, c % 2
        blocks = _own_blocks(r)
        idx = np.concatenate([np.arange(bk * 256, (bk + 1) * 256) for bk in blocks])
        rows.append(idx)
        m = dict(common)
        m.update(tabs[r])
        m["xq"] = np.ascontiguousarray(x[b][idx])
        m["xc"] = x[b]
        m["memx"] = mem[b]
        in_maps.append(m)
    ncr = _DBG['cores']
    resu = run_bass_kernel_spmd(nc, in_maps[:ncr], core_ids=list(range(ncr)))
    out = np.zeros((4, S_LEN, D), np.float32)
    for c in range(ncr):
        out[c // 2][rows[c]] = np.asarray(resu.results[c]["y_out"], dtype=np.float32)
    return out
```

```python
import contextlib
import numpy as np
import concourse.bass as bass
import concourse.mybir as mybir
from concourse.bass_utils import run_bass_kernel_spmd

F32 = mybir.dt.float32
BF16 = mybir.dt.bfloat16
AF = mybir.ActivationFunctionType
ALU = mybir.AluOpType
AX = mybir.AxisListType

D = 1024
S_LEN = 8192
NBLK = 32
H = 6
HC = 4
MEM = 256
INW = 9472
NG = 8
NCG = 16
OFF = dict(qa=0, ka=768, va=1536, ga=2304, ub=3072, vb=3840, gb=4608, qc=5376, gc=5888,
           ma=6400, mb=7424, mc=8448)
NEG = -30000.0
SCALE = 128 ** -0.5
EPS = 1e-6
WBUF = 6144
_DBG = dict(ncg=NCG, ng=NG, k=True, v=True, cores=8, fin=True, later=True, t1=True)


class Buf:
    def __init__(self, name, excl=False):
        self.name = name
        self.excl = excl
        self.last_w = None
        self.readers = []
        self.sem = None
        self.dma_count = 0


class Op:
    __slots__ = ("eng", "emit", "deps", "is_dma", "sem", "ticket", "need_signal")

    def __init__(self, eng, emit, is_dma):
        self.eng = eng
        self.emit = emit
        self.deps = []
        self.is_dma = is_dma
        self.sem = None
        self.ticket = None
        self.need_signal = False


class Sched:
    ENGS = ("pe", "act", "dve", "pool", "sp")

    def __init__(self, nc):
        self.nc = nc
        self.lists = {e: [] for e in self.ENGS}
        self.eng_sem = {}

    def _track(self, op, reads, writes):
        xr = [b for b in reads if b.excl]
        if xr:
            reads = [b for b in reads if not b.excl]
            writes = list(writes) + xr
        deps = []
        for b in reads:
            if b.last_w is not None:
                deps.append(b.last_w)
        for b in writes:
            if b.last_w is not None:
                deps.append(b.last_w)
            deps.extend(b.readers)
        seen = set()
        for d in deps:
            if d is op or id(d) in seen:
                continue
            seen.add(id(d))
            if (not d.is_dma) and (not op.is_dma) and d.eng == op.eng and op.eng == "pe":
                continue
            op.deps.append(d)
            if not d.is_dma:
                d.need_signal = True
        for b in reads:
            b.readers.append(op)
        for b in writes:
            b.last_w = op
            b.readers = []

    def op(self, eng, emit, reads=(), writes=()):
        o = Op(eng, emit, False)
        self._track(o, reads, writes)
        self.lists[eng].append(o)
        return o

    def dma(self, queue, emit, reads=(), writes=(), sem_buf=None):
        o = Op(queue, emit, True)
        sb = sem_buf if sem_buf is not None else writes[0]
        if sb.sem is None:
            sb.sem = self.nc.alloc_semaphore(name="d_" + sb.name)
        sb.dma_count += 16
        o.sem = sb.sem
        o.ticket = sb.dma_count
        self._track(o, reads, writes)
        self.lists[queue].append(o)
        return o

    def finalize(self, final_wait_ops=()):
        nc = self.nc
        for e in self.ENGS:
            self.eng_sem[e] = nc.alloc_semaphore(name="e_" + e)
        for e in self.ENGS:
            cnt = 0
            for o in self.lists[e]:
                if not o.is_dma and o.need_signal:
                    cnt += 1
                    o.sem = self.eng_sem[e]
                    o.ticket = cnt
        lists = self.lists
        final_wait_ops = list(final_wait_ops)

        def run(e, eng):
            waited = {}
            for o in lists[e]:
                need = {}
                for d in o.deps:
                    key = id(d.sem)
                    if waited.get(key, 0) >= d.ticket:
                        continue
                    if key not in need or need[key][1] < d.ticket:
                        need[key] = (d.sem, d.ticket)
                for key, (sem, tk) in need.items():
                    eng.wait_ge(sem, tk)
                    waited[key] = tk
                ins = o.emit(eng)
                if o.is_dma:
                    ins.then_inc(o.sem, 16)
                elif o.need_signal:
                    ins.then_inc(o.sem, 1)
            if e == "sp":
                for d in final_wait_ops:
                    if waited.get(id(d.sem), 0) >= d.ticket:
                        continue
                    eng.wait_ge(d.sem, d.ticket)
                    waited[id(d.sem)] = d.ticket

        with nc.Block() as block:
            @block.tensor
            def _(eng):
                run("pe", eng)

            @block.scalar
            def _(eng):
                run("act", eng)

            @block.vector
            def _(eng):
                run("dve", eng)

            @block.gpsimd
            def _(eng):
                run("pool", eng)

            @block.sync
            def _(eng):
                run("sp", eng)


def build_program():
    nc = bass.Bass("TRN2", target_bir_lowering=False)

    def din(name, shape, dt=F32):
        return nc.dram_tensor(name, list(shape), dt, kind="ExternalInput").ap()

    xq = din("xq", [NG * 512, D])
    xc = din("xc", [S_LEN, D])
    memx = din("memx", [MEM, D])
    w_in = din("w_in", [D, INW])
    w_mkv = din("w_mkv", [D, 2 * 512])
    w_a = din("w_a", [768, D])
    w_b = din("w_b", [768, D])
    w_c = din("w_c", [512, D])
    w_o = din("w_o", [D, D])
    gnt_d = din("gnt", [128, 8])
    gmt_d = din("gmt", [128, 8])
    fing_d = din("fing", [1, D])
    lng_d = din("lng", [1, 768])
    wsp_d = din("wsp", [6, 128, 128])
    bsp_d = din("bsp", [6, 1, 128])
    ident_d = din("ident", [128, 128])
    tril_d = din("tril", [128, 128])
    rm_d = din("rm", [32, 32])
    ohot_d = din("ohot", [32, 32 * 128])
    cosc_d = din("cosc", [32, S_LEN])
    sinc_d = din("sinc", [32, S_LEN])
    cosq_d = din("cosq", [32, NG * 512])
    sinq_d = din("sinq", [32, NG * 512])
    cmask_d = din("cmask", [128, 8 * 512])
    vneg_d = din("vneg", [128, 16 * 32])
    vmask_d = din("vmask", [128, 16 * 32])
    obias_d = din("obias", [128, 16 * 32])
    out_d = nc.dram_tensor("y_out", [NG * 512, D], F32, kind="ExternalOutput").ap()
    kscr = nc.dram_tensor("kscr", [H, 128, S_LEN], BF16, kind="Internal").ap()
    vscr = nc.dram_tensor("vscr", [S_LEN, 768], BF16, kind="Internal").ap()

    S = Sched(nc)
    es = contextlib.ExitStack()

    def SB(name, shape, dt):
        return es.enter_context(nc.sbuf_tensor("s_" + name, list(shape), dt))

    def PS(name, shape, dt):
        return es.enter_context(nc.psum_tensor("p_" + name, list(shape), dt))

    with es:
        identb = SB("identb", [128, 128], BF16); b_identb = Buf("identb")
        trilf = SB("trilf", [128, 128], F32); b_tril = Buf("tril")
        rmb = SB("rmb", [32, 32], BF16); b_rm = Buf("rm")
        ohot = SB("ohot", [32, 32, 128], BF16); b_ohot = Buf("ohot")
        gnt = SB("gnt", [128, 8], F32); b_gnt = Buf("gnt")
        gmt = SB("gmt", [128, 8], F32); b_gmt = Buf("gmt")
        fing = SB("fing", [128, D], F32); b_fing = Buf("fing")
        lng = SB("lng", [128, 768], F32); b_lng = Buf("lng")
        bspc = [SB("bspc%d" % c, [128, 128], F32) for c in range(6)]
        b_bspc = [Buf("bspc%d" % c) for c in range(6)]
        wspf = SB("wspf", [128, 128], F32); b_wspf = Buf("wspf")
        wspm = SB("wspm", [128, 128], BF16); b_wspm = Buf("wspm")
        wspT = SB("wspT", [128, 6, 128], BF16); b_wspT = Buf("wspT")
        cmask = SB("cmask", [128, 8, 512], BF16); b_cmask = Buf("cmask")
        vneg = SB("vneg", [128, 16, 32], F32); b_vneg = Buf("vneg")
        vmask = SB("vmask", [128, 16, 32], F32); b_vmask = Buf("vmask")
        obias = SB("obias", [128, 16, 32], F32); b_obias = Buf("obias")
        kmf = SB("kmf", [128, H, NBLK], F32); b_kmf = Buf("kmf")
        kmb = SB("kmb", [128, H, NBLK], BF16); b_kmb = Buf("kmb")
        mkT = SB("mkT", [128, HC, MEM], BF16); b_mkT = Buf("mkT")
        mvt = SB("mvt", [128, 2, HC, 130], BF16); b_mvt = Buf("mvt")
        NXT = 3
        xt = [SB("xt%d" % i, [128, D], F32) for i in range(NXT)]
        b_xt = [Buf("xt%d" % i) for i in range(NXT)]
        junk = SB("junk", [128, D], BF16); b_junk = Buf("junk")
        NST = 4
        st_t = SB("st", [128, NST, 8], F32); b_st_l = [Buf("st%d" % i) for i in range(NST)]
        xn = [SB("xn%d" % i, [128, D], BF16) for i in range(2)]
        b_xn = [Buf("xn%d" % i) for i in range(2)]
        hT = SB("hT", [128, 8, 512], BF16); b_hTt = [Buf("hT%d" % i) for i in range(4)]
        NW = 3
        wb = [SB("wb%d" % i, [128, WBUF], BF16) for i in range(NW)]
        b_wb = [Buf("wb%d" % i) for i in range(NW)]
        cost = [SB("cost%d" % i, [32, 512], F32) for i in range(2)]
        sint = [SB("sint%d" % i, [32, 512], F32) for i in range(2)]
        b_cos = [Buf("cos%d" % i) for i in range(2)]
        b_sin = [Buf("sin%d" % i) for i in range(2)]
        t1 = [SB("t1_%d" % i, [32, 512], F32) for i in range(2)]
        b_t1 = [Buf("t1_%d" % i) for i in range(2)]
        NPT = 4
        pT = [SB("pT%d" % i, [128, 512], BF16) for i in range(NPT)]
        b_pT = [Buf("pT%d" % i) for i in range(NPT)]
        vn = SB("vn", [128, 4, 768], BF16); b_vn = [Buf("vn%d" % i) for i in range(4)]
        vtmp = SB("vtmp", [128, 768], F32); b_vtmp = Buf("vtmp")
        bnst = SB("bnst", [128, 3, 6], F32); b_bnst = Buf("bnst")
        qT = SB("qT", [128, H, 512], BF16); b_qT = [Buf("qT%d" % h) for h in range(H)]
        sga = SB("sga", [128, H, 512], BF16); b_sga = [Buf("sga%d" % h) for h in range(H)]
        oaT = SB("oaT", [128, H, 512], BF16); b_oaT = [Buf("oaT%d" % h) for h in range(H)]
        g1 = SB("g1", [128, H, 512], BF16); b_g1 = [Buf("g1_%d" % h) for h in range(H)]
        g2 = SB("g2", [128, H, 512], BF16); b_g2 = [Buf("g2_%d" % h) for h in range(H)]
        g3 = SB("g3", [128, H, 512], BF16); b_g3 = [Buf("g3_%d" % h) for h in range(H)]
        selbT = SB("selbT", [32, H, 512], BF16); b_selbT = [Buf("selbT%d" % h) for h in range(H)]
        bsm = SB("bsm", [128, H, 32], F32); b_bsm = Buf("bsm")
        top8 = SB("top8", [128, H, 8], F32); b_top8 = Buf("top8")
        selv = SB("selv", [128, H, 32], F32); b_selv = Buf("selv")
        biasb = SB("biasb", [128, 4, H, 32], BF16); b_biasb = Buf("biasb")
        NKC = 2
        kch = [SB("kch%d" % i, [128, 2048], BF16) for i in range(NKC)]
        vch = [SB("vch%d" % i, [128, 16, 130], BF16) for i in range(NKC)]
        b_kch = [Buf("kch%d" % i) for i in range(NKC)]
        b_vch = [Buf("vch%d" % i) for i in range(NKC)]
        rden = SB("rden", [128, 4], F32); b_rden = Buf("rden")
        oa = SB("oa", [128, 4, 128], BF16); b_oa = Buf("oa")
        tmpb = SB("tmpb", [128, 512], F32); b_tmpb = Buf("tmpb")
        usg = SB("usg", [128, 512], F32); b_usg = Buf("usg")
        smg = SB("smg", [128, 4, 512], BF16); b_smg = [Buf("smg%d" % i) for i in range(4)]
        yacc = SB("yacc", [128, 8, 512], BF16); b_yacc = [Buf("yacc%d" % i) for i in range(8)]
        yT = yacc; b_yT = b_yacc
        ytmp = SB("ytmp", [128, 512], F32); b_ytmp = Buf("ytmp")
        pp = [PS("pp%d" % i, [128, 512], F32) for i in range(2)]
        b_pp = [Buf("pp%d" % i, True) for i in range(2)]
        ptr = PS("ptr", [128, 1024], BF16)
        _bp = Buf("ptr", True); b_ptr = [_bp, _bp]
        pmisc = PS("pmisc", [128, 512], F32); b_pmisc = Buf("pmisc", True)
        psc = [PS("psc%d" % i, [128, 512], F32) for i in range(2)]
        b_psc = [Buf("psc%d" % i, True) for i in range(2)]
        pp = pp + psc
        b_pp = b_pp + b_psc
        psc = pp
        b_psc = b_pp
        NB4 = 4
        pacc_t = [PS("pacc%d" % i, [128, 512], F32) for i in range(2)]
        pacc = [t[:, 0:260].rearrange("p (a b) -> p a b", b=130) for t in pacc_t]
        b_pacc = [Buf("pacc%d" % i, True) for i in range(2)]
        b_kscr = [[Buf("kscr%d_%d" % (h, cg)) for cg in range(NCG)] for h in range(H)]
        b_vscr = [[Buf("vscr%d_%d" % (cg, tt)) for tt in range(4)] for cg in range(NCG)]
        b_out = Buf("out")

        ctr = dict(pp=0, w=0, xt=0, xn=0, cs=0, t=0, pT=0, kc=0, psc=0, res=0, st=0)

        def rot(key, n):
            v = ctr[key] % n
            ctr[key] += 1
            return v

        def ld(queue, dst, src, bdst):
            S.dma(queue, lambda e: e.dma_start(out=dst, in_=src), writes=[bdst])

        ld("pool", identb[:], ident_d[:, :], b_identb)
        ld("sp", trilf[:], tril_d[:, :], b_tril)
        ld("pool", rmb[:], rm_d[:, :], b_rm)
        ld("pool", ohot[:], ohot_d.rearrange("k (j m) -> k j m", m=128), b_ohot)
        ld("sp", gnt[:], gnt_d[:, :], b_gnt)
        ld("sp", gmt[:], gmt_d[:, :], b_gmt)
        ld("sp", fing[:], fing_d.partition_broadcast(128), b_fing)
        ld("sp", lng[:], lng_d.partition_broadcast(128), b_lng)
        for c in range(6):
            ld("sp", bspc[c][:], bsp_d[c].partition_broadcast(128), b_bspc[c])
        ld("pool", cmask[:], cmask_d.rearrange("p (a b) -> p a b", b=512), b_cmask)
        ld("sp", vneg[:], vneg_d.rearrange("p (a b) -> p a b", b=32), b_vneg)
        ld("sp", vmask[:], vmask_d.rearrange("p (a b) -> p a b", b=32), b_vmask)
        ld("sp", obias[:], obias_d.rearrange("p (a b) -> p a b", b=32), b_obias)
        S.op("dve", lambda e: e.memset(kmf[:], 0.0), writes=[b_kmf])
        S.op("dve", lambda e: e.memset(mvt[:, :, :, 128:130], 1.0), writes=[b_mvt])
        for i in range(NKC):
            S.op("dve", (lambda i: lambda e: e.memset(vch[i][:, :, 128:130], 1.0))(i), writes=[b_vch[i]])
        for c in range(6):
            ld("sp", wspf[:], wsp_d[c], b_wspf)
            S.op("dve", lambda e: e.tensor_tensor(out=wspm[:], in0=wspf[:], in1=trilf[:], op=ALU.mult),
                 reads=[b_wspf, b_tril], writes=[b_wspm])
            S.op("pe", lambda e: e.transpose(out=ptr[:, 0:128], in_=wspm[:], identity=identb[:]),
                 reads=[b_wspm, b_identb], writes=[b_ptr[0]])
            S.op("dve", (lambda c: lambda e: e.tensor_copy(out=wspT[:, c, :], in_=ptr[:, 0:128]))(c),
                 reads=[b_ptr[0]], writes=[b_wspT])

        wscr = {}

        def precast(name, src_ap, kc, c0, ncols):
            t = nc.dram_tensor("wscr_" + name, [128, kc * ncols], BF16, kind="Internal").ap()
            b = Buf("wscr_" + name)
            src = src_ap.rearrange("(k p) c -> p k c", p=128)[:, :, c0:c0 + ncols]
            dstv = t.rearrange("p (k c) -> p k c", c=ncols)
            S.dma("pool", lambda e: e.dma_start(out=dstv, in_=src), writes=[b])
            wscr[name] = (t, b)

        def load_w(src_ap, kc, c0, ncols, name=None):
            i = rot("w", NW)
            view = wb[i][:, 0:kc * ncols].rearrange("p (k c) -> p k c", c=ncols)
            if name is not None and name in wscr:
                t, b = wscr[name]
                S.dma("pool", lambda e: e.dma_start(out=wb[i][:, 0:kc * ncols], in_=t[:, :]), reads=[b], writes=[b_wb[i]])
            else:
                src = src_ap.rearrange("(k p) c -> p k c", p=128)[:, :, c0:c0 + ncols]
                S.dma("pool", lambda e: e.dma_start(out=view, in_=src), writes=[b_wb[i]])
            return view, b_wb[i]

        def load_x(src_ap):
            xi = rot("xt", NXT)
            S.dma("sp", lambda e: e.dma_start(out=xt[xi][:], in_=src_ap), writes=[b_xt[xi]])
            return xi

        def norm_a(xi):
            si = rot("st", NST)
            st = st_t[:, si, :]
            b_st = b_st_l[si]
            S.op("act", lambda e: e.activation(out=junk[:], in_=xt[xi][:], func=AF.Square, accum_out=st[:, 0:1]),
                 reads=[b_xt[xi]], writes=[b_junk, b_st])
            S.op("act", lambda e: e.activation(out=st[:, 2:3], in_=st[:, 0:1], func=AF.Sqrt, scale=1.0 / D, bias=EPS),
                 reads=[b_st], writes=[b_st])
            S.op("dve", lambda e: e.reciprocal(out=st[:, 3:4], in_=st[:, 2:3]), reads=[b_st], writes=[b_st])
            ni = rot("xn", 2)
            S.op("dve", lambda e: e.tensor_scalar(out=xn[ni][:], in0=xt[xi][:], scalar1=st[:, 3:4], scalar2=None, op0=ALU.mult),
                 reads=[b_xt[xi], b_st], writes=[b_xn[ni]])
            return ni

        def norm_b(ni, tt, gvec, b_gvec):
            def tr(e):
                ins = None
                for k in range(8):
                    ins = e.transpose(out=ptr[:, k * 128:(k + 1) * 128], in_=xn[ni][:, k * 128:(k + 1) * 128],
                                      identity=identb[:])
                return ins
            S.op("pe", tr, reads=[b_xn[ni], b_identb], writes=[b_ptr[0], b_ptr[1]])
            S.op("dve", lambda e: e.tensor_tensor(
                out=hT[:, :, tt * 128:(tt + 1) * 128],
                in0=ptr[:, :].rearrange("p (k t) -> p k t", t=128),
                in1=gvec[:, :].unsqueeze(2).broadcast_to([128, 8, 128]), op=ALU.mult),
                reads=[b_ptr[0], b_ptr[1], b_gvec], writes=[b_hTt[tt]])

        def make_hT(src_rows, ntiles, gvec, b_gvec):
            xs = [load_x(src_rows(tt)) for tt in range(min(3, ntiles))]
            nis = {0: norm_a(xs[0])}
            for tt in range(ntiles):
                if tt == 0 and ntiles > 3:
                    xs.append(load_x(src_rows(3)))
                if tt + 1 < ntiles:
                    nis[tt + 1] = norm_a(xs[tt + 1])
                norm_b(nis[tt], tt, gvec, b_gvec)

        def proj_fm(wview, b_w, c0, ntok=512, kcn=8, rhs_fn=None, rhs_bufs=None):
            bi = rot("pp", NB4)
            if rhs_fn is None:
                rhs_fn = lambda k: hT[:, k, 0:ntok]
                rhs_bufs = b_hTt

            def f(e):
                ins = None
                for k in range(kcn):
                    ins = e.matmul(pp[bi][:, 0:ntok], lhsT=wview[:, k, c0:c0 + 128], rhs=rhs_fn(k),
                                   start=(k == 0), stop=(k == kcn - 1))
                return ins
            S.op("pe", f, reads=[b_w] + list(rhs_bufs), writes=[b_pp[bi]])
            return bi

        def rope_evac(bi, dst, b_dst, csi, pending):
            S.op("act", lambda e: e.activation(out=dst, in_=pp[bi][:], func=AF.Copy), reads=[b_pp[bi]], writes=[b_dst])
            ti = rot("t", 2)
            if _DBG['t1'] == 1:
                S.op("dve", lambda e: e.tensor_tensor(out=t1[ti][:], in0=pp[bi][0:32, :], in1=cost[csi][:], op=ALU.mult),
                     reads=[b_pp[bi], b_cos[csi]], writes=[b_t1[ti]])
            elif _DBG['t1'] == 2:
                S.op("dve", lambda e: e.tensor_tensor(out=t1[ti][:], in0=fing[0:32, 0:512], in1=cost[csi][:], op=ALU.mult),
                     reads=[b_fing, b_cos[csi]], writes=[b_t1[ti]])
            elif _DBG['t1'] == 3:
                S.op("dve", lambda e: e.tensor_tensor(out=t1[ti][:], in0=pp[bi][0:32, :], in1=fing[0:32, 0:512], op=ALU.mult),
                     reads=[b_pp[bi], b_fing], writes=[b_t1[ti]])

            def later():
                S.op("pe", lambda e: e.matmul(pmisc[0:32, :], lhsT=rmb[:, :], rhs=dst[0:32, :], start=True, stop=True),
                     reads=[b_rm, b_dst], writes=[b_pmisc])
                S.op("dve", lambda e: e.tensor_tensor(out=pmisc[0:32, :], in0=pmisc[0:32, :], in1=sint[csi][:], op=ALU.mult),
                     reads=[b_pmisc, b_sin[csi]], writes=[b_pmisc])
                S.op("dve", lambda e: e.tensor_tensor(out=dst[0:32, :], in0=pmisc[0:32, :], in1=t1[ti][:], op=ALU.add),
                     reads=[b_t1[ti], b_pmisc], writes=[b_dst])
            pending.append(later)

        def load_cs(cos_d, sin_d, col0):
            csi = rot("cs", 2)
            S.dma("sp", lambda e: e.dma_start(out=cost[csi][:], in_=cos_d[:, col0:col0 + 512]), writes=[b_cos[csi]])
            S.dma("sp", lambda e: e.dma_start(out=sint[csi][:], in_=sin_d[:, col0:col0 + 512]), writes=[b_sin[csi]])
            return csi

        def attn_core(ktiles, q_ap, b_q, post, loaders=None):
            n = len(ktiles)
            slots = {}

            def qk(i):
                kt = ktiles[i]
                bi = rot("psc", NB4)
                slots[i] = bi
                ex = kt["extra"]

                def f(e):
                    ins = e.matmul(psc[bi][:], lhsT=kt["k"], rhs=q_ap, start=True, stop=(len(ex) == 0))
                    for j, (l, r, _) in enumerate(ex):
                        ins = e.matmul(psc[bi][:], lhsT=l, rhs=r, start=False, stop=(j == len(ex) - 1))
                    return ins
                rd = list(kt["kb"]) + [b_q]
                for (_, _, bufs) in ex:
                    rd += list(bufs)
                S.op("pe", f, reads=rd, writes=[b_psc[bi]])

            def ex_pv(i):
                kt = ktiles[i]
                bi = slots[i]
                pi = rot("pT", NPT)
                S.op("act", lambda e: e.activation(out=pT[pi][:], in_=psc[bi][:], func=AF.Exp, scale=SCALE),
                     reads=[b_psc[bi]], writes=[b_pT[pi]])
                return pi

            def pv(i, pi):
                kt = ktiles[i]

                def f(e):
                    ins = None
                    for qt in range(4):
                        ins = e.matmul(pacc[qt // 2][:, qt % 2, 0:129], lhsT=pT[pi][:, qt * 128:(qt + 1) * 128],
                                       rhs=kt["v"], start=(i == 0 and qt % 2 == 0), stop=(i == n - 1),
                                       skip_group_check=True)
                    return ins
                S.op("pe", f, reads=[b_pT[pi]] + list(kt["vb"]), writes=[b_pacc[0], b_pacc[1]])

            loaded = set()

            def need(c):
                if loaders is not None and c is not None and c < len(loaders) and c not in loaded:
                    loaded.add(c)
                    loaders[c]()

            AH = 3

            def prologue():
                for j in range(min(AH, n)):
                    need(ktiles[j].get("chunk"))
                    qk(j)

            def body():
                for i in range(n):
                    pi = ex_pv(i)
                    if i + AH < n:
                        need(ktiles[i + AH].get("chunk"))
                        qk(i + AH)
                    pv(i, pi)
                    c = ktiles[i].get("chunk")
                    if c is not None and i % 16 == 1:
                        need(c + 1)
            return prologue, body, post

        def run_stages(stages):
            for i, (pro, body, post) in enumerate(stages):
                if i == 0:
                    pro()
                body()
                if i + 1 < len(stages):
                    stages[i + 1][0]()
                post()

        def attn_post(gate_ap, b_gate, dst_ap, b_dst):
            for bk in range(2):
                S.op("dve", (lambda bk: lambda e: e.reciprocal(out=rden[:, 2 * bk:2 * bk + 2].unsqueeze(2),
                                                              in_=pacc[bk][:, :, 128:129]))(bk),
                     reads=[b_pacc[bk]], writes=[b_rden])
            for qt in range(4):
                if qt < 2:
                    S.op("dve", (lambda qt: lambda e: e.tensor_scalar(out=oa[:, qt, :], in0=pacc[qt // 2][:, qt % 2, 0:128],
                                                                      scalar1=rden[:, qt:qt + 1], scalar2=None,
                                                                      op0=ALU.mult))(qt),
                         reads=[b_pacc[qt // 2], b_rden], writes=[b_oa])
                else:
                    S.op("act", (lambda qt: lambda e: e.activation(out=oa[:, qt, :], in_=pacc[qt // 2][:, qt % 2, 0:128],
                                                                   func=AF.Copy, scale=rden[:, qt:qt + 1]))(qt),
                         reads=[b_pacc[qt // 2], b_rden], writes=[b_oa])

            def tr(e):
                ins = None
                for qt in range(4):
                    ins = e.transpose(out=ptr[:, qt * 128:(qt + 1) * 128], in_=oa[:, qt, :], identity=identb[:])
                return ins
            S.op("pe", tr, reads=[b_oa, b_identb], writes=[b_ptr[0]])
            S.op("dve", lambda e: e.tensor_tensor(out=dst_ap, in0=ptr[:, 0:512], in1=gate_ap, op=ALU.mult),
                 reads=[b_ptr[0], b_gate], writes=[b_dst])

        make_hT(lambda tt: memx[tt * 128:(tt + 1) * 128, :], 2, gmt, b_gmt)
        wk_v, wk_b = load_w(w_mkv, 8, 0, 512)
        wv_v, wv_b = load_w(w_mkv, 8, 512, 512)
        for hc in range(HC):
            bi = proj_fm(wk_v, wk_b, hc * 128, ntok=MEM)
            S.op("act", (lambda bi, hc: lambda e: e.activation(out=mkT[:, hc, :], in_=pp[bi][:, 0:MEM], func=AF.Copy))(bi, hc),
                 reads=[b_pp[bi]], writes=[b_mkT])
        for mt in range(2):
            bi = rot("pp", 2)

            def f(e, bi=bi, mt=mt, wv_v=wv_v):
                ins = None
                for k in range(8):
                    ins = e.matmul(pp[bi][:, :], lhsT=hT[:, k, mt * 128:(mt + 1) * 128], rhs=wv_v[:, k, :],
                                   start=(k == 0), stop=(k == 7))
                return ins
            S.op("pe", f, reads=b_hTt + [wv_b], writes=[b_pp[bi]])
            S.op("act", (lambda bi, mt: lambda e: e.activation(
                out=mvt[:, mt, :, 0:128], in_=pp[bi][:, :].rearrange("p (h d) -> p h d", d=128), func=AF.Copy))(bi, mt),
                reads=[b_pp[bi]], writes=[b_mvt])

        wk_v, wk_b = load_w(w_in, 8, OFF["ka"], 768)
        wv_v, wv_b = load_w(w_in, 8, OFF["va"], 768)
        for nm in ("qa", "ga", "ub", "gb", "vb"):
            precast(nm, w_in, 8, OFF[nm], 768)
        precast("m%d_0" % OFF["mb"], w_in, 8, OFF["mb"], 512)
        precast("w_b", w_b, 6, 0, D)
        precast("m%d_1" % OFF["mb"], w_in, 8, OFF["mb"] + 512, 512)
        for nm in ("qc", "gc"):
            precast(nm, w_in, 8, OFF[nm], 512)
        precast("m%d_0" % OFF["mc"], w_in, 8, OFF["mc"], 512)
        precast("w_c", w_c, 4, 0, D)
        precast("m%d_1" % OFF["mc"], w_in, 8, OFF["mc"] + 512, 512)
        precast("m%d_0" % OFF["ma"], w_in, 8, OFF["ma"], 512)
        precast("w_a", w_a, 6, 0, D)
        precast("m%d_1" % OFF["ma"], w_in, 8, OFF["ma"] + 512, 512)
        precast("wo0", w_o, 8, 0, 512)
        precast("wo1", w_o, 8, 512, 512)
        def xc_rows(cg, tt):
            return xc[cg * 512 + tt * 128: cg * 512 + (tt + 1) * 128, :]

        if _DBG['ncg'] > 0:
            make_hT(lambda tt: xc_rows(0, tt), 4, gnt, b_gnt)
        for cg in range(_DBG['ncg']):
            nxt = cg + 1 < _DBG['ncg']
            pre_x = [load_x(xc_rows(cg + 1, tt)) for tt in range(3)] if nxt else []
            csi = load_cs(cosc_d, sinc_d, cg * 512)
            prev = None
            for h in range(H if _DBG['k'] else 0):
                bi = proj_fm(wk_v, wk_b, h * 128)
                pi = rot("pT", NPT)
                cur = []
                rope_evac(bi, pT[pi][:, :], b_pT[pi], csi, cur)

                def fin(pi=pi, h=h, cg=cg):
                    S.op("dve", lambda e: e.tensor_reduce(out=kmf[:, h, 2 * cg:2 * cg + 2],
                                                          in_=pT[pi][:, :].rearrange("p (a b) -> p a b", b=256),
                                                          axis=AX.X, op=ALU.add),
                         reads=[b_pT[pi]], writes=[b_kmf])
                    S.dma("sp", lambda e: e.dma_start(out=kscr[h, :, cg * 512:(cg + 1) * 512], in_=pT[pi][:, :]),
                          reads=[b_pT[pi]], writes=[b_kscr[h][cg]], sem_buf=b_pT[pi])
                if not _DBG['later']:
                    cur = []
                if _DBG['fin']:
                    cur.append(fin)
                if prev is not None:
                    for fn in prev:
                        fn()
                prev = cur
            for fn in (prev or []):
                fn()
            pre_n = {0: norm_a(pre_x[0])} if nxt else {}
            for tt in range(4 if _DBG['v'] else 0):
                p0, p1 = 2 * (tt % 2), 2 * (tt % 2) + 1

                def f(e, tt=tt, wv_v=wv_v, p0=p0, p1=p1):
                    ins = None
                    for k in range(8):
                        ins = e.matmul(pp[p0][:, :], lhsT=hT[:, k, tt * 128:(tt + 1) * 128], rhs=wv_v[:, k, 0:512],
                                       start=(k == 0), stop=(k == 7))
                    for k in range(8):
                        ins = e.matmul(pp[p1][:, 0:256], lhsT=hT[:, k, tt * 128:(tt + 1) * 128], rhs=wv_v[:, k, 512:768],
                                       start=(k == 0), stop=(k == 7))
                    return ins
                S.op("pe", f, reads=[b_hTt[tt], wv_b], writes=[b_pp[p0], b_pp[p1]])
                S.op("act", (lambda tt, p0: lambda e: e.activation(out=vn[:, tt, 0:512], in_=pp[p0][:, :], func=AF.Copy))(tt, p0),
                     reads=[b_pp[p0]], writes=[b_vn[tt]])
                S.op("act", (lambda tt, p1: lambda e: e.activation(out=vn[:, tt, 512:768], in_=pp[p1][:, 0:256], func=AF.Copy))(tt, p1),
                     reads=[b_pp[p1]], writes=[b_vn[tt]])
                S.dma("sp", (lambda tt, cg: lambda e: e.dma_start(
                    out=vscr[cg * 512 + tt * 128: cg * 512 + (tt + 1) * 128, :], in_=vn[:, tt, :]))(tt, cg),
                    reads=[b_vn[tt]], writes=[b_vscr[cg][tt]], sem_buf=b_vn[tt])
                if nxt:
                    norm_b(pre_n[tt], tt, gnt, b_gnt)
                    if tt == 0:
                        pre_x.append(load_x(xc_rows(cg + 1, 3)))
                    if tt + 1 < 4:
                        pre_n[tt + 1] = norm_a(pre_x[tt + 1])
        S.op("dve", lambda e: e.tensor_scalar(out=kmb[:], in0=kmf[:], scalar1=1.0 / 256, scalar2=None, op0=ALU.mult),
             reads=[b_kmf], writes=[b_kmb])

        out_dmas = []
        for g in _DBG.get('glist', range(_DBG['ng'])):
            make_hT(lambda tt, g=g: xq[g * 512 + tt * 128: g * 512 + (tt + 1) * 128, :], 4, gnt, b_gnt)
            csi = load_cs(cosq_d, sinq_d, g * 512)
            wv_, wb_ = load_w(w_in, 8, OFF["qa"], 768, "qa")
            prev = None
            for h in range(H):
                bi = proj_fm(wv_, wb_, h * 128)
                cur = []
                rope_evac(bi, qT[:, h, :], b_qT[h], csi, cur)
                if prev is not None:
                    for fn in prev:
                        fn()
                prev = cur
            for fn in prev:
                fn()
            fillers = []
            w_ga = load_w(w_in, 8, OFF["ga"], 768, "ga")
            w_ub = load_w(w_in, 8, OFF["ub"], 768, "ub")
            w_gb = load_w(w_in, 8, OFF["gb"], 768, "gb")

            def mk_filler(wv_wb, c, dst, b_dst, func):
                def fn():
                    bi = proj_fm(wv_wb[0], wv_wb[1], c * 128)
                    S.op("act", lambda e: e.activation(out=dst, in_=pp[bi][:], func=func), reads=[b_pp[bi]], writes=[b_dst])
                return fn
            for c in range(6):
                fillers.append(mk_filler(w_ga, c, sga[:, c, :], b_sga[c], AF.Silu))
            for c in range(6):
                fillers.append(mk_filler(w_ub, c, g1[:, c, :], b_g1[c], AF.Copy))
            for c in range(6):
                fillers.append(mk_filler(w_gb, c, g2[:, c, :], b_g2[c], AF.Silu))
            for qt in range(4):
                gs = 2 * g + qt // 2

                def f(e, qt=qt):
                    ins = None
                    for h in range(H):
                        ins = e.matmul(pmisc[:, h * 32:(h + 1) * 32], lhsT=qT[:, h, qt * 128:(qt + 1) * 128],
                                       rhs=kmb[:, h, :], start=True, stop=True)
                    return ins
                S.op("pe", f, reads=b_qT + [b_kmb], writes=[b_pmisc])
                for h in range(H):
                    S.op("dve", (lambda h, gs: lambda e: e.tensor_tensor(out=bsm[:, h, :], in0=pmisc[:, h * 32:(h + 1) * 32],
                                                                         in1=vneg[:, gs, :], op=ALU.add))(h, gs),
                         reads=[b_pmisc, b_vneg], writes=[b_bsm])
                for h in range(H):
                    S.op("dve", (lambda h: lambda e: e.max(out=top8[:, h, :], in_=bsm[:, h, :]))(h),
                         reads=[b_bsm], writes=[b_top8])
                for h in range(H):
                    S.op("dve", (lambda h, gs: lambda e: e.scalar_tensor_tensor(
                        out=selv[:, h, :], in0=bsm[:, h, :], scalar=top8[:, h, 2:3], in1=vmask[:, gs, :],
                        op0=ALU.is_ge, op1=ALU.mult))(h, gs),
                        reads=[b_bsm, b_top8, b_vmask], writes=[b_selv])
                for h in range(H):
                    S.op("dve", (lambda h, gs, qt: lambda e: e.scalar_tensor_tensor(
                        out=biasb[:, qt, h, :], in0=selv[:, h, :], scalar=-NEG, in1=obias[:, gs, :],
                        op0=ALU.mult, op1=ALU.add))(h, gs, qt),
                        reads=[b_selv, b_obias], writes=[b_biasb])
                for _ in range(4):
                    if fillers:
                        fillers.pop(0)()
            while fillers:
                fillers.pop(0)()
            for h in range(H):
                half = h % 2

                def tr(e, h=h, half=half):
                    ins = None
                    for qt in range(4):
                        ins = e.transpose(out=ptr[0:32, half * 512 + qt * 128: half * 512 + (qt + 1) * 128],
                                          in_=biasb[:, qt, h, :], identity=identb[:])
                    return ins
                S.op("pe", tr, reads=[b_biasb, b_identb], writes=[b_ptr[half]])
                S.op("act", (lambda h, half: lambda e: e.activation(out=selbT[:, h, :],
                                                                    in_=ptr[0:32, half * 512:(half + 1) * 512],
                                                                    func=AF.Copy))(h, half),
                     reads=[b_ptr[half]], writes=[b_selbT[h]])
            nkt = 8 * (g + 1)
            stages = []
            for h in range(H):
                ktiles = []
                loaders = []
                for c0 in range(0, nkt, 16):
                    nt = min(16, nkt - c0)
                    ki = rot("kc", NKC)

                    def loader(ki=ki, h=h, c0=c0, nt=nt):
                        S.dma("sp", lambda e: e.dma_start(
                            out=kch[ki][:, 0:nt * 128], in_=kscr[h, :, c0 * 128:(c0 + nt) * 128]),
                            reads=[b_kscr[h][cg] for cg in range(c0 // 4, (c0 + nt + 3) // 4)], writes=[b_kch[ki]])
                        S.dma("sp", lambda e: e.dma_start(
                            out=vch[ki][:, 0:nt, 0:128],
                            in_=vscr.rearrange("(t p) c -> p t c", p=128)[:, c0:c0 + nt, h * 128:(h + 1) * 128]),
                            reads=[b_vscr[cg][t4] for cg in range(c0 // 4, (c0 + nt + 3) // 4) for t4 in range(4)],
                            writes=[b_vch[ki]])
                    loaders.append(loader)
                    for j in range(nt):
                        kt = c0 + j
                        extra = [(ohot[:, kt // 2, :], selbT[:, h, :], [b_ohot, b_selbT[h]])]
                        if kt >= nkt - 8:
                            extra.append((identb[:, :], cmask[:, kt - (nkt - 8), :], [b_identb, b_cmask]))
                        ktiles.append(dict(k=kch[ki][:, j * 128:(j + 1) * 128], kb=[b_kch[ki]],
                                           v=vch[ki][:, j, 0:129], vb=[b_vch[ki]], extra=extra, chunk=c0 // 16))
                stages.append(attn_core(ktiles, qT[:, h, :], b_qT[h],
                                        (lambda h: lambda: attn_post(sga[:, h, :], b_sga[h], oaT[:, h, :], b_oaT[h]))(h),
                                        loaders=loaders))
            run_stages(stages)

            wv_, wb_ = load_w(w_in, 8, OFF["vb"], 768, "vb")
            for tt in range(4):
                p0, p1 = 2 * (tt % 2), 2 * (tt % 2) + 1

                def f(e, tt=tt, wv_=wv_, p0=p0, p1=p1):
                    ins = None
                    for k in range(8):
                        ins = e.matmul(pp[p0][:, :], lhsT=hT[:, k, tt * 128:(tt + 1) * 128], rhs=wv_[:, k, 0:512],
                                       start=(k == 0), stop=(k == 7))
                    for k in range(8):
                        ins = e.matmul(pp[p1][:, 0:256], lhsT=hT[:, k, tt * 128:(tt + 1) * 128], rhs=wv_[:, k, 512:768],
                                       start=(k == 0), stop=(k == 7))
                    return ins
                S.op("pe", f, reads=[b_hTt[tt], wb_], writes=[b_pp[p0], b_pp[p1]])
                S.op("dve", lambda e, p0=p0, p1=p1: e.bn_stats(out=bnst[:, 0, :], in_=pp[p0][:, 0:256]), reads=[b_pp[p0]], writes=[b_bnst])
                S.op("dve", lambda e, p0=p0, p1=p1: e.bn_stats(out=bnst[:, 1, :], in_=pp[p0][:, 256:512]), reads=[b_pp[p0]], writes=[b_bnst])
                S.op("dve", lambda e, p0=p0, p1=p1: e.bn_stats(out=bnst[:, 2, :], in_=pp[p1][:, 0:256]), reads=[b_pp[p1]], writes=[b_bnst])
                si = rot("st", NST)
                st = st_t[:, si, :]
                b_st = b_st_l[si]
                S.op("dve", (lambda st: lambda e: e.bn_aggr(out=st[:, 4:6], in_=bnst[:, :, :].rearrange("p a b -> p (a b)")))(st),
                     reads=[b_bnst], writes=[b_st])
                S.op("act", (lambda st: lambda e: e.activation(out=st[:, 6:7], in_=st[:, 5:6], func=AF.Sqrt, scale=1.0, bias=EPS))(st),
                     reads=[b_st], writes=[b_st])
                S.op("dve", (lambda st: lambda e: e.reciprocal(out=st[:, 7:8], in_=st[:, 6:7]))(st), reads=[b_st], writes=[b_st])
                S.op("dve", (lambda st, p0=p0, p1=p1: lambda e: e.tensor_scalar(out=vtmp[:, 0:512], in0=pp[p0][:, :], scalar1=st[:, 4:5], scalar2=st[:, 7:8],
                                                                  op0=ALU.subtract, op1=ALU.mult))(st),
                     reads=[b_pp[p0], b_st], writes=[b_vtmp])
                S.op("dve", (lambda st, p0=p0, p1=p1: lambda e: e.tensor_scalar(out=vtmp[:, 512:768], in0=pp[p1][:, 0:256], scalar1=st[:, 4:5], scalar2=st[:, 7:8],
                                                                  op0=ALU.subtract, op1=ALU.mult))(st),
                     reads=[b_pp[p1], b_st], writes=[b_vtmp])
                S.op("pool", (lambda tt: lambda e: e.tensor_tensor(out=vn[:, tt, :], in0=vtmp[:, :], in1=lng[:, :], op=ALU.mult))(tt),
                     reads=[b_vtmp, b_lng], writes=[b_vn[tt]])
            for c in range(6):
                bi = rot("pp", 2)

                def f(e, bi=bi, c=c):
                    ins = None
                    for tt in range(4):
                        ins = e.matmul(pp[bi][:, tt * 128:(tt + 1) * 128], lhsT=vn[:, tt, c * 128:(c + 1) * 128],
                                       rhs=wspT[:, c, :], start=True, stop=True)
                    return ins
                S.op("pe", f, reads=b_vn + [b_wspT], writes=[b_pp[bi]])
                S.op("dve", (lambda bi, c: lambda e: e.tensor_tensor(
                    out=tmpb[:, :].rearrange("p (a b) -> p a b", b=128),
                    in0=pp[bi][:, :].rearrange("p (a b) -> p a b", b=128),
                    in1=bspc[c][:, :].unsqueeze(1).broadcast_to([128, 4, 128]), op=ALU.add))(bi, c),
                    reads=[b_pp[bi], b_bspc[c]], writes=[b_tmpb])
                S.op("pool", (lambda c: lambda e: e.tensor_tensor(out=usg[:, :], in0=g1[:, c, :], in1=g2[:, c, :], op=ALU.mult))(c),
                     reads=[b_g1[c], b_g2[c]], writes=[b_usg])
                S.op("dve", (lambda c: lambda e: e.tensor_tensor(out=g3[:, c, :], in0=tmpb[:, :], in1=usg[:, :], op=ALU.mult))(c),
                     reads=[b_tmpb, b_usg], writes=[b_g3[c]])

            def merge(first, last, moff, wsrc, kcn, srcT, b_src, wname):
                wv3, wb3 = load_w(wsrc, kcn, 0, D, wname)
                for half in range(2):
                    wv2, wb2 = load_w(w_in, 8, moff + half * 512, 512, "m%d_%d" % (moff, half))
                    for c in range(4):
                        bi = proj_fm(wv2, wb2, c * 128)
                        S.op("act", (lambda bi, c: lambda e: e.activation(out=smg[:, c, :], in_=pp[bi][:], func=AF.Sigmoid))(bi, c),
                             reads=[b_pp[bi]], writes=[b_smg[c]])
                    for c in range(4):
                        oc = half * 4 + c
                        bi = proj_fm(wv3, wb3, oc * 128, kcn=kcn, rhs_fn=lambda k: srcT[:, k, :], rhs_bufs=b_src[0:kcn])
                        if first:
                            S.op("dve", (lambda bi, oc, c: lambda e: e.tensor_tensor(out=yacc[:, oc, :], in0=pp[bi][:], in1=smg[:, c, :],
                                                                                     op=ALU.mult))(bi, oc, c),
                                 reads=[b_pp[bi], b_smg[c]], writes=[b_yacc[oc]])
                        else:
                            S.op("dve", (lambda bi, c: lambda e: e.tensor_tensor(out=ytmp[:, :], in0=pp[bi][:], in1=smg[:, c, :],
                                                                                 op=ALU.mult))(bi, c),
                                 reads=[b_pp[bi], b_smg[c]], writes=[b_ytmp])
                            S.op("dve", (lambda oc: lambda e: e.tensor_tensor(out=yacc[:, oc, :], in0=yacc[:, oc, :], in1=ytmp[:, :],
                                                                              op=ALU.add))(oc),
                                 reads=[b_yacc[oc], b_ytmp], writes=[b_yacc[oc]])

            merge(True, False, OFF["mb"], w_b, 6, g3, b_g3, "w_b")

            wv_, wb_ = load_w(w_in, 8, OFF["qc"], 512, "qc")
            for c in range(HC):
                bi = proj_fm(wv_, wb_, c * 128)
                S.op("act", (lambda bi, c: lambda e: e.activation(out=g1[:, c, :], in_=pp[bi][:], func=AF.Copy))(bi, c),
                     reads=[b_pp[bi]], writes=[b_g1[c]])
            wv_, wb_ = load_w(w_in, 8, OFF["gc"], 512, "gc")
            for c in range(HC):
                bi = proj_fm(wv_, wb_, c * 128)
                S.op("act", (lambda bi, c: lambda e: e.activation(out=g2[:, c, :], in_=pp[bi][:], func=AF.Silu))(bi, c),
                     reads=[b_pp[bi]], writes=[b_g2[c]])
            stages = []
            for hc in range(HC):
                ktiles = [dict(k=mkT[:, hc, mt * 128:(mt + 1) * 128], kb=[b_mkT], v=mvt[:, mt, hc, 0:129], vb=[b_mvt], extra=[])
                          for mt in range(2)]
                stages.append(attn_core(ktiles, g1[:, hc, :], b_g1[hc],
                                        (lambda hc: lambda: attn_post(g2[:, hc, :], b_g2[hc], g3[:, hc, :], b_g3[hc]))(hc)))
            run_stages(stages)
            merge(False, False, OFF["mc"], w_c, 4, g3, b_g3, "w_c")
            merge(False, True, OFF["ma"], w_a, 6, oaT, b_oaT, "w_a")

            wo0, wob0 = load_w(w_o, 8, 0, 512, "wo0")
            wo1, wob1 = load_w(w_o, 8, 512, 512, "wo1")
            for tt in range(4):
                xi = rot("xt", NXT)
                S.dma("sp", (lambda xi, tt, g: lambda e: e.dma_start(
                    out=xt[xi][:], in_=xq[g * 512 + tt * 128: g * 512 + (tt + 1) * 128, :]))(xi, tt, g),
                    writes=[b_xt[xi]])

                p0, p1 = 2 * (tt % 2), 2 * (tt % 2) + 1

                def f(e, tt=tt, wo0=wo0, wo1=wo1, p0=p0, p1=p1):
                    ins = None
                    for k in range(8):
                        ins = e.matmul(pp[p0][:, :], lhsT=yT[:, k, tt * 128:(tt + 1) * 128], rhs=wo0[:, k, :],
                                       start=(k == 0), stop=(k == 7))
                    for k in range(8):
                        ins = e.matmul(pp[p1][:, :], lhsT=yT[:, k, tt * 128:(tt + 1) * 128], rhs=wo1[:, k, :],
                                       start=(k == 0), stop=(k == 7))
                    return ins
                S.op("pe", f, reads=b_yT + [wob0, wob1], writes=[b_pp[p0], b_pp[p1]])
                ri = xi
                S.op("dve", (lambda ri, xi, p0=p0, p1=p1: lambda e: e.tensor_tensor(out=xt[ri][:, 0:512], in0=pp[p0][:, :], in1=xt[xi][:, 0:512],
                                                                      op=ALU.add))(ri, xi),
                     reads=[b_pp[p0], b_xt[xi]], writes=[b_xt[ri]])
                S.op("dve", (lambda ri, xi, p0=p0, p1=p1: lambda e: e.tensor_tensor(out=xt[ri][:, 512:1024], in0=pp[p1][:, :], in1=xt[xi][:, 512:1024],
                                                                      op=ALU.add))(ri, xi),
                     reads=[b_pp[p1], b_xt[xi]], writes=[b_xt[ri]])
                si = rot("st", NST)
                st = st_t[:, si, :]
                b_st = b_st_l[si]
                S.op("act", (lambda ri, st: lambda e: e.activation(out=junk[:], in_=xt[ri][:], func=AF.Square, accum_out=st[:, 0:1]))(ri, st),
                     reads=[b_xt[ri]], writes=[b_junk, b_st])
                S.op("act", (lambda st: lambda e: e.activation(out=st[:, 2:3], in_=st[:, 0:1], func=AF.Sqrt, scale=1.0 / D, bias=EPS))(st),
                     reads=[b_st], writes=[b_st])
                S.op("dve", (lambda st: lambda e: e.reciprocal(out=st[:, 3:4], in_=st[:, 2:3]))(st), reads=[b_st], writes=[b_st])
                S.op("dve", (lambda ri, st: lambda e: e.scalar_tensor_tensor(out=xt[ri][:], in0=xt[ri][:], scalar=st[:, 3:4], in1=fing[:],
                                                                             op0=ALU.mult, op1=ALU.mult))(ri, st),
                     reads=[b_xt[ri], b_st, b_fing], writes=[b_xt[ri]])
                od = S.dma("sp", (lambda ri, tt, g: lambda e: e.dma_start(
                    out=out_d[g * 512 + tt * 128: g * 512 + (tt + 1) * 128, :], in_=xt[ri][:]))(ri, tt, g),
                    reads=[b_xt[ri]], writes=[Buf("o")], sem_buf=b_xt[ri])
                out_dmas.append(od)

        S.finalize(out_dmas[-2:])
    return nc


def _own_blocks(r):
    blocks = []
    for g in range(NG):
        blocks.append(4 * g + (0 if r == 0 else 1))
        blocks.append(4 * g + (3 if r == 0 else 2))
    return blocks


def _host_tables(r):
    blocks = _own_blocks(r)
    half = 8
    inv_freq = (500000.0 ** (-np.arange(0, 32, 2, dtype=np.float32) / 32)).astype(np.float32)
    qpos = np.concatenate([np.arange(b * 256, (b + 1) * 256) for b in blocks])

    def cs(pos):
        ang = pos.astype(np.float32)[:, None] * inv_freq[None, :]
        c = np.cos(ang).astype(np.float32).T
        s = np.sin(ang).astype(np.float32).T
        return np.ascontiguousarray(np.concatenate([c, c], 0)), np.ascontiguousarray(np.concatenate([s, s], 0))
    cosq, sinq = cs(qpos)
    cosc, sinc = cs(np.arange(S_LEN))
    cm = np.zeros((128, 8, 512), np.float32)
    for kt in range(8):
        kpos = kt * 128 + np.arange(128)
        for slot in range(2):
            rel = blocks[slot] * 256 + np.arange(256)
            cm[:, kt, slot * 256:(slot + 1) * 256] = np.where(kpos[:, None] <= rel[None, :], 0.0, NEG)
    vneg = np.zeros((128, 16, 32), np.float32)
    vmask = np.zeros((128, 16, 32), np.float32)
    obias = np.full((128, 16, 32), NEG, np.float32)
    for i, b in enumerate(blocks):
        vmask[:, i, :b] = 1.0
        vneg[:, i, b:] = -1e30
        obias[:, i, b] = 0.0
    return dict(cosq=cosq, sinq=sinq, cosc=cosc, sinc=sinc, cmask=cm.reshape(128, -1),
                vneg=vneg.reshape(128, -1), vmask=vmask.reshape(128, -1), obias=obias.reshape(128, -1))


_NC_CACHE = {}


def kernel(x, mem, norm_g, mem_norm_g, final_norm_g, w_in, w_mem_kv, gmlp_ln_g, w_spatial, b_spatial,
           w_branch_a, w_branch_b, w_branch_c, w_out):
    f = lambda a: np.ascontiguousarray(np.asarray(a, dtype=np.float32))
    x = f(x); mem = f(mem)
    if "nc" not in _NC_CACHE:
        _NC_CACHE["nc"] = build_program()
    nc = _NC_CACHE["nc"]
    rm = np.zeros((32, 32), np.float32)
    for i in range(16):
        rm[i + 16, i] = -1.0
        rm[i, i + 16] = 1.0
    ohot = np.zeros((32, 32, 128), np.float32)
    for j in range(32):
        ohot[j, j, :] = 1.0
    common = dict(
        w_in=f(w_in[0]), w_mkv=f(w_mem_kv[0]), w_a=f(w_branch_a[0]), w_b=f(w_branch_b[0]), w_c=f(w_branch_c[0]),
        w_o=f(w_out[0]),
        gnt=f(np.asarray(norm_g[0]).reshape(8, 128).T), gmt=f(np.asarray(mem_norm_g[0]).reshape(8, 128).T),
        fing=f(np.asarray(final_norm_g).reshape(1, D)), lng=f(np.asarray(gmlp_ln_g[0]).reshape(1, 768)),
        wsp=f(w_spatial[0]), bsp=f(np.asarray(b_spatial[0]).reshape(6, 1, 128)),
        ident=np.eye(128, dtype=np.float32), tril=np.tril(np.ones((128, 128), np.float32)),
        rm=rm, ohot=ohot.reshape(32, -1),
    )
    tabs = [_host_tables(0), _host_tables(1)]
    in_maps = []
    rows = []
    for c in range(8):
        b, r = c // 2, c % 2
        blocks = _own_blocks(r)
        idx = np.concatenate([np.arange(bk * 256, (bk + 1) * 256) for bk in blocks])
        rows.append(idx)
        m = dict(common)
        m.update(tabs[r])
        m["xq"] = np.ascontiguousarray(x[b][idx])
        m["xc"] = x[b]
        m["memx"] = mem[b]
        in_maps.append(m)
    ncr = _DBG['cores']
    resu = run_bass_kernel_spmd(nc, in_maps[:ncr], core_ids=list(range(ncr)))
    out = np.zeros((4, S_LEN, D), np.float32)
    for c in range(ncr):
        out[c // 2][rows[c]] = np.asarray(resu.results[c]["y_out"], dtype=np.float32)
    return out
```

```python
import contextlib
import numpy as np
import concourse.bass as bass
import concourse.mybir as mybir
from concourse.bass_utils import run_bass_kernel_spmd

F32 = mybir.dt.float32
BF16 = mybir.dt.bfloat16
AF = mybir.ActivationFunctionType
ALU = mybir.AluOpType
AX = mybir.AxisListType

D = 1024
S_LEN = 8192
NBLK = 32
H = 6
HC = 4
MEM = 256
INW = 9472
NG = 8
NCG = 16
OFF = dict(qa=0, ka=768, va=1536, ga=2304, ub=3072, vb=3840, gb=4608, qc=5376, gc=5888,
           ma=6400, mb=7424, mc=8448)
NEG = -30000.0
SCALE = 128 ** -0.5
EPS = 1e-6
WBUF = 6144
_DBG = dict(ncg=NCG, ng=NG, k=True, v=True, cores=8, fin=True, later=True, t1=True)


class Buf:
    def __init__(self, name, excl=False):
        self.name = name
        self.excl = excl
        self.last_w = None
        self.readers = []
        self.sem = None
        self.dma_count = 0


class Op:
    __slots__ = ("eng", "emit", "deps", "is_dma", "sem", "ticket", "need_signal")

    def __init__(self, eng, emit, is_dma):
        self.eng = eng
        self.emit = emit
        self.deps = []
        self.is_dma = is_dma
        self.sem = None
        self.ticket = None
        self.need_signal = False


class Sched:
    ENGS = ("pe", "act", "dve", "pool", "sp")

    def __init__(self, nc):
        self.nc = nc
        self.lists = {e: [] for e in self.ENGS}
        self.eng_sem = {}

    def _track(self, op, reads, writes):
        xr = [b for b in reads if b.excl]
        if xr:
            reads = [b for b in reads if not b.excl]
            writes = list(writes) + xr
        deps = []
        for b in reads:
            if b.last_w is not None:
                deps.append(b.last_w)
        for b in writes:
            if b.last_w is not None:
                deps.append(b.last_w)
            deps.extend(b.readers)
        seen = set()
        for d in deps:
            if d is op or id(d) in seen:
                continue
            seen.add(id(d))
            if (not d.is_dma) and (not op.is_dma) and d.eng == op.eng and op.eng == "pe":
                continue
            op.deps.append(d)
            if not d.is_dma:
                d.need_signal = True
        for b in reads:
            b.readers.append(op)
        for b in writes:
            b.last_w = op
            b.readers = []

    def op(self, eng, emit, reads=(), writes=()):
        o = Op(eng, emit, False)
        self._track(o, reads, writes)
        self.lists[eng].append(o)
        return o

    def dma(self, queue, emit, reads=(), writes=(), sem_buf=None):
        o = Op(queue, emit, True)
        sb = sem_buf if sem_buf is not None else writes[0]
        if sb.sem is None:
            sb.sem = self.nc.alloc_semaphore(name="d_" + sb.name)
        sb.dma_count += 16
        o.sem = sb.sem
        o.ticket = sb.dma_count
        self._track(o, reads, writes)
        self.lists[queue].append(o)
        return o

    def finalize(self, final_wait_ops=()):
        nc = self.nc
        for e in self.ENGS:
            self.eng_sem[e] = nc.alloc_semaphore(name="e_" + e)
        for e in self.ENGS:
            cnt = 0
            for o in self.lists[e]:
                if not o.is_dma and o.need_signal:
                    cnt += 1
                    o.sem = self.eng_sem[e]
                    o.ticket = cnt
        lists = self.lists
        final_wait_ops = list(final_wait_ops)

        def run(e, eng):
            waited = {}
            for o in lists[e]:
                need = {}
                for d in o.deps:
                    key = id(d.sem)
                    if waited.get(key, 0) >= d.ticket:
                        continue
                    if key not in need or need[key][1] < d.ticket:
                        need[key] = (d.sem, d.ticket)
                for key, (sem, tk) in need.items():
                    eng.wait_ge(sem, tk)
                    waited[key] = tk
                ins = o.emit(eng)
                if o.is_dma:
                    ins.then_inc(o.sem, 16)
                elif o.need_signal:
                    ins.then_inc(o.sem, 1)
            if e == "sp":
                for d in final_wait_ops:
                    if waited.get(id(d.sem), 0) >= d.ticket:
                        continue
                    eng.wait_ge(d.sem, d.ticket)
                    waited[id(d.sem)] = d.ticket

        with nc.Block() as block:
            @block.tensor
            def _(eng):
                run("pe", eng)

            @block.scalar
            def _(eng):
                run("act", eng)

            @block.vector
            def _(eng):
                run("dve", eng)

            @block.gpsimd
            def _(eng):
                run("pool", eng)

            @block.sync
            def _(eng):
                run("sp", eng)


def build_program():
    nc = bass.Bass("TRN2", target_bir_lowering=False)

    def din(name, shape, dt=F32):
        return nc.dram_tensor(name, list(shape), dt, kind="ExternalInput").ap()

    xq = din("xq", [NG * 512, D])
    xc = din("xc", [S_LEN, D])
    memx = din("memx", [MEM, D])
    w_in = din("w_in", [D, INW])
    w_mkv = din("w_mkv", [D, 2 * 512])
    w_a = din("w_a", [768, D])
    w_b = din("w_b", [768, D])
    w_c = din("w_c", [512, D])
    w_o = din("w_o", [D, D])
    gnt_d = din("gnt", [128, 8])
    gmt_d = din("gmt", [128, 8])
    fing_d = din("fing", [1, D])
    lng_d = din("lng", [1, 768])
    wsp_d = din("wsp", [6, 128, 128])
    bsp_d = din("bsp", [6, 1, 128])
    ident_d = din("ident", [128, 128])
    tril_d = din("tril", [128, 128])
    rm_d = din("rm", [32, 32])
    ohot_d = din("ohot", [32, 32 * 128])
    cosc_d = din("cosc", [32, S_LEN])
    sinc_d = din("sinc", [32, S_LEN])
    cosq_d = din("cosq", [32, NG * 512])
    sinq_d = din("sinq", [32, NG * 512])
    cmask_d = din("cmask", [128, 8 * 512])
    vneg_d = din("vneg", [128, 16 * 32])
    vmask_d = din("vmask", [128, 16 * 32])
    obias_d = din("obias", [128, 16 * 32])
    out_d = nc.dram_tensor("y_out", [NG * 512, D], F32, kind="ExternalOutput").ap()
    kscr = nc.dram_tensor("kscr", [H, 128, S_LEN], BF16, kind="Internal").ap()
    vscr = nc.dram_tensor("vscr", [S_LEN, 768], BF16, kind="Internal").ap()

    S = Sched(nc)
    es = contextlib.ExitStack()

    def SB(name, shape, dt):
        return es.enter_context(nc.sbuf_tensor("s_" + name, list(shape), dt))

    def PS(name, shape, dt):
        return es.enter_context(nc.psum_tensor("p_" + name, list(shape), dt))

    with es:
        identb = SB("identb", [128, 128], BF16); b_identb = Buf("identb")
        trilf = SB("trilf", [128, 128], F32); b_tril = Buf("tril")
        rmb = SB("rmb", [32, 32], BF16); b_rm = Buf("rm")
        ohot = SB("ohot", [32, 32, 128], BF16); b_ohot = Buf("ohot")
        gnt = SB("gnt", [128, 8], F32); b_gnt = Buf("gnt")
        gmt = SB("gmt", [128, 8], F32); b_gmt = Buf("gmt")
        fing = SB("fing", [128, D], F32); b_fing = Buf("fing")
        lng = SB("lng", [128, 768], F32); b_lng = Buf("lng")
        bspc = [SB("bspc%d" % c, [128, 128], F32) for c in range(6)]
        b_bspc = [Buf("bspc%d" % c) for c in range(6)]
        wspf = SB("wspf", [128, 128], F32); b_wspf = Buf("wspf")
        wspm = SB("wspm", [128, 128], BF16); b_wspm = Buf("wspm")
        wspT = SB("wspT", [128, 6, 128], BF16); b_wspT = Buf("wspT")
        cmask = SB("cmask", [128, 8, 512], BF16); b_cmask = Buf("cmask")
        vneg = SB("vneg", [128, 16, 32], F32); b_vneg = Buf("vneg")
        vmask = SB("vmask", [128, 16, 32], F32); b_vmask = Buf("vmask")
        obias = SB("obias", [128, 16, 32], F32); b_obias = Buf("obias")
        kmf = SB("kmf", [128, H, NBLK], F32); b_kmf = Buf("kmf")
        kmb = SB("kmb", [128, H, NBLK], BF16); b_kmb = Buf("kmb")
        mkT = SB("mkT", [128, HC, MEM], BF16); b_mkT = Buf("mkT")
        mvt = SB("mvt", [128, 2, HC, 130], BF16); b_mvt = Buf("mvt")
        NXT = 3
        xt = [SB("xt%d" % i, [128, D], F32) for i in range(NXT)]
        b_xt = [Buf("xt%d" % i) for i in range(NXT)]
        junk = SB("junk", [128, D], BF16); b_junk = Buf("junk")
        NST = 4
        st_t = SB("st", [128, NST, 8], F32); b_st_l = [Buf("st%d" % i) for i in range(NST)]
        xn = [SB("xn%d" % i, [128, D], BF16) for i in range(2)]
        b_xn = [Buf("xn%d" % i) for i in range(2)]
        hT = SB("hT", [128, 8, 512], BF16); b_hTt = [Buf("hT%d" % i) for i in range(4)]
        NW = 3
        wb = [SB("wb%d" % i, [128, WBUF], BF16) for i in range(NW)]
        b_wb = [Buf("wb%d" % i) for i in range(NW)]
        cost = [SB("cost%d" % i, [32, 512], F32) for i in range(2)]
        sint = [SB("sint%d" % i, [32, 512], F32) for i in range(2)]
        b_cos = [Buf("cos%d" % i) for i in range(2)]
        b_sin = [Buf("sin%d" % i) for i in range(2)]
        t1 = [SB("t1_%d" % i, [32, 512], F32) for i in range(2)]
        b_t1 = [Buf("t1_%d" % i) for i in range(2)]
        NPT = 4
        pT = [SB("pT%d" % i, [128, 512], BF16) for i in range(NPT)]
        b_pT = [Buf("pT%d" % i) for i in range(NPT)]
        vn = SB("vn", [128, 4, 768], BF16); b_vn = [Buf("vn%d" % i) for i in range(4)]
        vtmp = SB("vtmp", [128, 768], F32); b_vtmp = Buf("vtmp")
        bnst = SB("bnst", [128, 3, 6], F32); b_bnst = Buf("bnst")
        qT = SB("qT", [128, H, 512], BF16); b_qT = [Buf("qT%d" % h) for h in range(H)]
        sga = SB("sga", [128, H, 512], BF16); b_sga = [Buf("sga%d" % h) for h in range(H)]
        oaT = SB("oaT", [128, H, 512], BF16); b_oaT = [Buf("oaT%d" % h) for h in range(H)]
        g1 = SB("g1", [128, H, 512], BF16); b_g1 = [Buf("g1_%d" % h) for h in range(H)]
        g2 = SB("g2", [128, H, 512], BF16); b_g2 = [Buf("g2_%d" % h) for h in range(H)]
        g3 = SB("g3", [128, H, 512], BF16); b_g3 = [Buf("g3_%d" % h) for h in range(H)]
        selbT = SB("selbT", [32, H, 512], BF16); b_selbT = [Buf("selbT%d" % h) for h in range(H)]
        bsm = SB("bsm", [128, H, 32], F32); b_bsm = Buf("bsm")
        top8 = SB("top8", [128, H, 8], F32); b_top8 = Buf("top8")
        selv = SB("selv", [128, H, 32], F32); b_selv = Buf("selv")
        biasb = SB("biasb", [128, 4, H, 32], BF16); b_biasb = Buf("biasb")
        NKC = 2
        kch = [SB("kch%d" % i, [128, 2048], BF16) for i in range(NKC)]
        vch = [SB("vch%d" % i, [128, 16, 130], BF16) for i in range(NKC)]
        b_kch = [Buf("kch%d" % i) for i in range(NKC)]
        b_vch = [Buf("vch%d" % i) for i in range(NKC)]
        onesb = SB("onesb", [128, 128], BF16); b_onesb = Buf("onesb")
        tmpb = SB("tmpb", [128, 512], F32); b_tmpb = Buf("tmpb")
        usg = SB("usg", [128, 512], F32); b_usg = Buf("usg")
        smg = SB("smg", [128, 4, 512], BF16); b_smg = [Buf("smg%d" % i) for i in range(4)]
        yacc = SB("yacc", [128, 8, 512], BF16); b_yacc = [Buf("yacc%d" % i) for i in range(8)]
        yT = yacc; b_yT = b_yacc
        ytmp = SB("ytmp", [128, 512], F32); b_ytmp = Buf("ytmp")
        pp = [PS("pp%d" % i, [128, 512], F32) for i in range(2)]
        b_pp = [Buf("pp%d" % i, True) for i in range(2)]
        ptr = PS("ptr", [128, 1024], BF16)
        _bp = Buf("ptr", True); b_ptr = [_bp, _bp]
        pmisc = PS("pmisc", [128, 512], F32); b_pmisc = Buf("pmisc", True)
        psc = [PS("psc%d" % i, [128, 512], F32) for i in range(2)]
        b_psc = [Buf("psc%d" % i, True) for i in range(2)]
        pp = pp + psc
        b_pp = b_pp + b_psc
        psc = pp
        b_psc = b_pp
        NB4 = 4
        pacc_t = [PS("pacc%d" % i, [128, 512], F32) for i in range(2)]
        pacc = [t[:, 0:260].rearrange("p (a b) -> p a b", b=130) for t in pacc_t]
        b_pacc = [Buf("pacc%d" % i, True) for i in range(2)]
        b_kscr = [[Buf("kscr%d_%d" % (h, cg)) for cg in range(NCG)] for h in range(H)]
        b_vscr = [[Buf("vscr%d_%d" % (cg, tt)) for tt in range(4)] for cg in range(NCG)]
        b_out = Buf("out")

        ctr = dict(pp=0, w=0, xt=0, xn=0, cs=0, t=0, pT=0, kc=0, psc=0, res=0, st=0)

        def rot(key, n):
            v = ctr[key] % n
            ctr[key] += 1
            return v

        def ld(queue, dst, src, bdst):
            S.dma(queue, lambda e: e.dma_start(out=dst, in_=src), writes=[bdst])

        ld("pool", identb[:], ident_d[:, :], b_identb)
        ld("sp", trilf[:], tril_d[:, :], b_tril)
        ld("pool", rmb[:], rm_d[:, :], b_rm)
        ld("pool", ohot[:], ohot_d.rearrange("k (j m) -> k j m", m=128), b_ohot)
        ld("sp", gnt[:], gnt_d[:, :], b_gnt)
        ld("sp", gmt[:], gmt_d[:, :], b_gmt)
        ld("sp", fing[:], fing_d.partition_broadcast(128), b_fing)
        ld("sp", lng[:], lng_d.partition_broadcast(128), b_lng)
        for c in range(6):
            ld("sp", bspc[c][:], bsp_d[c].partition_broadcast(128), b_bspc[c])
        ld("pool", cmask[:], cmask_d.rearrange("p (a b) -> p a b", b=512), b_cmask)
        ld("sp", vneg[:], vneg_d.rearrange("p (a b) -> p a b", b=32), b_vneg)
        ld("sp", vmask[:], vmask_d.rearrange("p (a b) -> p a b", b=32), b_vmask)
        ld("sp", obias[:], obias_d.rearrange("p (a b) -> p a b", b=32), b_obias)
        S.op("dve", lambda e: e.memset(kmf[:], 0.0), writes=[b_kmf])
        S.op("dve", lambda e: e.memset(onesb[:], 1.0), writes=[b_onesb])
        S.op("dve", lambda e: e.memset(mvt[:, :, :, 128:130], 1.0), writes=[b_mvt])
        for i in range(NKC):
            S.op("dve", (lambda i: lambda e: e.memset(vch[i][:, :, 128:130], 1.0))(i), writes=[b_vch[i]])
        for c in range(6):
            ld("sp", wspf[:], wsp_d[c], b_wspf)
            S.op("dve", lambda e: e.tensor_tensor(out=wspm[:], in0=wspf[:], in1=trilf[:], op=ALU.mult),
                 reads=[b_wspf, b_tril], writes=[b_wspm])
            S.op("pe", lambda e: e.transpose(out=ptr[:, 0:128], in_=wspm[:], identity=identb[:]),
                 reads=[b_wspm, b_identb], writes=[b_ptr[0]])
            S.op("dve", (lambda c: lambda e: e.tensor_copy(out=wspT[:, c, :], in_=ptr[:, 0:128]))(c),
                 reads=[b_ptr[0]], writes=[b_wspT])

        wscr = {}

        def precast(name, src_ap, kc, c0, ncols):
            t = nc.dram_tensor("wscr_" + name, [128, kc * ncols], BF16, kind="Internal").ap()
            b = Buf("wscr_" + name)
            src = src_ap.rearrange("(k p) c -> p k c", p=128)[:, :, c0:c0 + ncols]
            dstv = t.rearrange("p (k c) -> p k c", c=ncols)
            S.dma("pool", lambda e: e.dma_start(out=dstv, in_=src), writes=[b])
            wscr[name] = (t, b)

        def load_w(src_ap, kc, c0, ncols, name=None):
            i = rot("w", NW)
            view = wb[i][:, 0:kc * ncols].rearrange("p (k c) -> p k c", c=ncols)
            if name is not None and name in wscr:
                t, b = wscr[name]
                S.dma("pool", lambda e: e.dma_start(out=wb[i][:, 0:kc * ncols], in_=t[:, :]), reads=[b], writes=[b_wb[i]])
            else:
                src = src_ap.rearrange("(k p) c -> p k c", p=128)[:, :, c0:c0 + ncols]
                S.dma("pool", lambda e: e.dma_start(out=view, in_=src), writes=[b_wb[i]])
            return view, b_wb[i]

        def load_x(src_ap):
            xi = rot("xt", NXT)
            S.dma("sp", lambda e: e.dma_start(out=xt[xi][:], in_=src_ap), writes=[b_xt[xi]])
            return xi

        def norm_a(xi):
            si = rot("st", NST)
            st = st_t[:, si, :]
            b_st = b_st_l[si]
            S.op("act", lambda e: e.activation(out=junk[:], in_=xt[xi][:], func=AF.Square, accum_out=st[:, 0:1]),
                 reads=[b_xt[xi]], writes=[b_junk, b_st])
            S.op("act", lambda e: e.activation(out=st[:, 2:3], in_=st[:, 0:1], func=AF.Sqrt, scale=1.0 / D, bias=EPS),
                 reads=[b_st], writes=[b_st])
            S.op("dve", lambda e: e.reciprocal(out=st[:, 3:4], in_=st[:, 2:3]), reads=[b_st], writes=[b_st])
            ni = rot("xn", 2)
            S.op("dve", lambda e: e.tensor_scalar(out=xn[ni][:], in0=xt[xi][:], scalar1=st[:, 3:4], scalar2=None, op0=ALU.mult),
                 reads=[b_xt[xi], b_st], writes=[b_xn[ni]])
            return ni

        def norm_b(ni, tt, gvec, b_gvec):
            def tr(e):
                ins = None
                for k in range(8):
                    ins = e.transpose(out=ptr[:, k * 128:(k + 1) * 128], in_=xn[ni][:, k * 128:(k + 1) * 128],
                                      identity=identb[:])
                return ins
            S.op("pe", tr, reads=[b_xn[ni], b_identb], writes=[b_ptr[0], b_ptr[1]])
            S.op("dve", lambda e: e.tensor_tensor(
                out=hT[:, :, tt * 128:(tt + 1) * 128],
                in0=ptr[:, :].rearrange("p (k t) -> p k t", t=128),
                in1=gvec[:, :].unsqueeze(2).broadcast_to([128, 8, 128]), op=ALU.mult),
                reads=[b_ptr[0], b_ptr[1], b_gvec], writes=[b_hTt[tt]])

        def make_hT(src_rows, ntiles, gvec, b_gvec):
            xs = [load_x(src_rows(tt)) for tt in range(min(3, ntiles))]
            nis = {0: norm_a(xs[0])}
            for tt in range(ntiles):
                if tt == 0 and ntiles > 3:
                    xs.append(load_x(src_rows(3)))
                if tt + 1 < ntiles:
                    nis[tt + 1] = norm_a(xs[tt + 1])
                norm_b(nis[tt], tt, gvec, b_gvec)

        def proj_fm(wview, b_w, c0, ntok=512, kcn=8, rhs_fn=None, rhs_bufs=None):
            bi = rot("pp", NB4)
            if rhs_fn is None:
                rhs_fn = lambda k: hT[:, k, 0:ntok]
                rhs_bufs = b_hTt

            def f(e):
                ins = None
                for k in range(kcn):
                    ins = e.matmul(pp[bi][:, 0:ntok], lhsT=wview[:, k, c0:c0 + 128], rhs=rhs_fn(k),
                                   start=(k == 0), stop=(k == kcn - 1))
                return ins
            S.op("pe", f, reads=[b_w] + list(rhs_bufs), writes=[b_pp[bi]])
            return bi

        def rope_evac(bi, dst, b_dst, csi, pending):
            S.op("act", lambda e: e.activation(out=dst, in_=pp[bi][:], func=AF.Copy), reads=[b_pp[bi]], writes=[b_dst])
            ti = rot("t", 2)
            if _DBG['t1'] == 1:
                S.op("dve", lambda e: e.tensor_tensor(out=t1[ti][:], in0=pp[bi][0:32, :], in1=cost[csi][:], op=ALU.mult),
                     reads=[b_pp[bi], b_cos[csi]], writes=[b_t1[ti]])
            elif _DBG['t1'] == 2:
                S.op("dve", lambda e: e.tensor_tensor(out=t1[ti][:], in0=fing[0:32, 0:512], in1=cost[csi][:], op=ALU.mult),
                     reads=[b_fing, b_cos[csi]], writes=[b_t1[ti]])
            elif _DBG['t1'] == 3:
                S.op("dve", lambda e: e.tensor_tensor(out=t1[ti][:], in0=pp[bi][0:32, :], in1=fing[0:32, 0:512], op=ALU.mult),
                     reads=[b_pp[bi], b_fing], writes=[b_t1[ti]])

            def later():
                S.op("pe", lambda e: e.matmul(pmisc[0:32, :], lhsT=rmb[:, :], rhs=dst[0:32, :], start=True, stop=True),
                     reads=[b_rm, b_dst], writes=[b_pmisc])
                S.op("dve", lambda e: e.tensor_tensor(out=pmisc[0:32, :], in0=pmisc[0:32, :], in1=sint[csi][:], op=ALU.mult),
                     reads=[b_pmisc, b_sin[csi]], writes=[b_pmisc])
                S.op("dve", lambda e: e.tensor_tensor(out=dst[0:32, :], in0=pmisc[0:32, :], in1=t1[ti][:], op=ALU.add),
                     reads=[b_t1[ti], b_pmisc], writes=[b_dst])
            pending.append(later)

        def load_cs(cos_d, sin_d, col0):
            csi = rot("cs", 2)
            S.dma("sp", lambda e: e.dma_start(out=cost[csi][:], in_=cos_d[:, col0:col0 + 512]), writes=[b_cos[csi]])
            S.dma("sp", lambda e: e.dma_start(out=sint[csi][:], in_=sin_d[:, col0:col0 + 512]), writes=[b_sin[csi]])
            return csi

        def attn_core(ktiles, q_ap, b_q, post, loaders=None):
            n = len(ktiles)
            slots = {}

            def qk(i):
                kt = ktiles[i]
                bi = rot("psc", NB4)
                slots[i] = bi
                ex = kt["extra"]

                def f(e):
                    ins = e.matmul(psc[bi][:], lhsT=kt["k"], rhs=q_ap, start=True, stop=(len(ex) == 0))
                    for j, (l, r, _) in enumerate(ex):
                        ins = e.matmul(psc[bi][:], lhsT=l, rhs=r, start=False, stop=(j == len(ex) - 1))
                    return ins
                rd = list(kt["kb"]) + [b_q]
                for (_, _, bufs) in ex:
                    rd += list(bufs)
                S.op("pe", f, reads=rd, writes=[b_psc[bi]])

            def ex_pv(i):
                kt = ktiles[i]
                bi = slots[i]
                pi = rot("pT", NPT)
                S.op("act", lambda e: e.activation(out=pT[pi][:], in_=psc[bi][:], func=AF.Exp, scale=SCALE),
                     reads=[b_psc[bi]], writes=[b_pT[pi]])
                return pi

            def pv(i, pi):
                kt = ktiles[i]

                def f(e):
                    e.matmul(pacc_t[0][:, :], lhsT=kt["v"], rhs=pT[pi][:, :], start=(i == 0), stop=(i == n - 1))
                    return e.matmul(pacc_t[1][:, :], lhsT=onesb[:, :], rhs=pT[pi][:, :], start=(i == 0), stop=(i == n - 1))
                S.op("pe", f, reads=[b_pT[pi], b_onesb] + list(kt["vb"]), writes=[b_pacc[0], b_pacc[1]])

            loaded = set()

            def need(c):
                if loaders is not None and c is not None and c < len(loaders) and c not in loaded:
                    loaded.add(c)
                    loaders[c]()

            AH = 3

            def prologue():
                for j in range(min(AH, n)):
                    need(ktiles[j].get("chunk"))
                    qk(j)

            def body():
                for i in range(n):
                    pi = ex_pv(i)
                    if i + AH < n:
                        need(ktiles[i + AH].get("chunk"))
                        qk(i + AH)
                    pv(i, pi)
                    c = ktiles[i].get("chunk")
                    if c is not None and i % 16 == 1:
                        need(c + 1)
            return prologue, body, post

        def run_stages(stages):
            for i, (pro, body, post) in enumerate(stages):
                if i == 0:
                    pro()
                body()
                if i + 1 < len(stages):
                    stages[i + 1][0]()
                post()

        def attn_post(gate_ap, b_gate, dst_ap, b_dst):
            S.op("act", lambda e: e.activation(out=tmpb[:, :], in_=pacc_t[1][:, :], func=AF.Ln), reads=[b_pacc[1]], writes=[b_tmpb])
            S.op("act", lambda e: e.activation(out=tmpb[:, :], in_=tmpb[:, :], func=AF.Exp, scale=-1.0), reads=[b_tmpb], writes=[b_tmpb])
            S.op("dve", lambda e: e.tensor_tensor(out=tmpb[:, :], in0=pacc_t[0][:, :], in1=tmpb[:, :], op=ALU.mult),
                 reads=[b_pacc[0], b_tmpb], writes=[b_tmpb])
            S.op("dve", lambda e: e.tensor_tensor(out=dst_ap, in0=tmpb[:, :], in1=gate_ap, op=ALU.mult),
                 reads=[b_tmpb, b_gate], writes=[b_dst])

        make_hT(lambda tt: memx[tt * 128:(tt + 1) * 128, :], 2, gmt, b_gmt)
        wk_v, wk_b = load_w(w_mkv, 8, 0, 512)
        wv_v, wv_b = load_w(w_mkv, 8, 512, 512)
        for hc in range(HC):
            bi = proj_fm(wk_v, wk_b, hc * 128, ntok=MEM)
            S.op("act", (lambda bi, hc: lambda e: e.activation(out=mkT[:, hc, :], in_=pp[bi][:, 0:MEM], func=AF.Copy))(bi, hc),
                 reads=[b_pp[bi]], writes=[b_mkT])
        for mt in range(2):
            bi = rot("pp", 2)

            def f(e, bi=bi, mt=mt, wv_v=wv_v):
                ins = None
                for k in range(8):
                    ins = e.matmul(pp[bi][:, :], lhsT=hT[:, k, mt * 128:(mt + 1) * 128], rhs=wv_v[:, k, :],
                                   start=(k == 0), stop=(k == 7))
                return ins
            S.op("pe", f, reads=b_hTt + [wv_b], writes=[b_pp[bi]])
            S.op("act", (lambda bi, mt: lambda e: e.activation(
                out=mvt[:, mt, :, 0:128], in_=pp[bi][:, :].rearrange("p (h d) -> p h d", d=128), func=AF.Copy))(bi, mt),
                reads=[b_pp[bi]], writes=[b_mvt])

        wk_v, wk_b = load_w(w_in, 8, OFF["ka"], 768)
        wv_v, wv_b = load_w(w_in, 8, OFF["va"], 768)
        for nm in ("qa", "ga", "ub", "gb", "vb"):
            precast(nm, w_in, 8, OFF[nm], 768)
        precast("m%d_0" % OFF["mb"], w_in, 8, OFF["mb"], 512)
        precast("w_b", w_b, 6, 0, D)
        precast("m%d_1" % OFF["mb"], w_in, 8, OFF["mb"] + 512, 512)
        for nm in ("qc", "gc"):
            precast(nm, w_in, 8, OFF[nm], 512)
        precast("m%d_0" % OFF["mc"], w_in, 8, OFF["mc"], 512)
        precast("w_c", w_c, 4, 0, D)
        precast("m%d_1" % OFF["mc"], w_in, 8, OFF["mc"] + 512, 512)
        precast("m%d_0" % OFF["ma"], w_in, 8, OFF["ma"], 512)
        precast("w_a", w_a, 6, 0, D)
        precast("m%d_1" % OFF["ma"], w_in, 8, OFF["ma"] + 512, 512)
        precast("wo0", w_o, 8, 0, 512)
        precast("wo1", w_o, 8, 512, 512)
        def xc_rows(cg, tt):
            return xc[cg * 512 + tt * 128: cg * 512 + (tt + 1) * 128, :]

        if _DBG['ncg'] > 0:
            make_hT(lambda tt: xc_rows(0, tt), 4, gnt, b_gnt)
        for cg in range(_DBG['ncg']):
            nxt = cg + 1 < _DBG['ncg']
            pre_x = [load_x(xc_rows(cg + 1, tt)) for tt in range(3)] if nxt else []
            csi = load_cs(cosc_d, sinc_d, cg * 512)
            prev = None
            for h in range(H if _DBG['k'] else 0):
                bi = proj_fm(wk_v, wk_b, h * 128)
                pi = rot("pT", NPT)
                cur = []
                rope_evac(bi, pT[pi][:, :], b_pT[pi], csi, cur)

                def fin(pi=pi, h=h, cg=cg):
                    S.op("dve", lambda e: e.tensor_reduce(out=kmf[:, h, 2 * cg:2 * cg + 2],
                                                          in_=pT[pi][:, :].rearrange("p (a b) -> p a b", b=256),
                                                          axis=AX.X, op=ALU.add),
                         reads=[b_pT[pi]], writes=[b_kmf])
                    S.dma("sp", lambda e: e.dma_start(out=kscr[h, :, cg * 512:(cg + 1) * 512], in_=pT[pi][:, :]),
                          reads=[b_pT[pi]], writes=[b_kscr[h][cg]], sem_buf=b_pT[pi])
                if not _DBG['later']:
                    cur = []
                if _DBG['fin']:
                    cur.append(fin)
                if prev is not None:
                    for fn in prev:
                        fn()
                prev = cur
            for fn in (prev or []):
                fn()
            pre_n = {0: norm_a(pre_x[0])} if nxt else {}
            for tt in range(4 if _DBG['v'] else 0):
                p0, p1 = 2 * (tt % 2), 2 * (tt % 2) + 1

                def f(e, tt=tt, wv_v=wv_v, p0=p0, p1=p1):
                    ins = None
                    for k in range(8):
                        ins = e.matmul(pp[p0][:, :], lhsT=hT[:, k, tt * 128:(tt + 1) * 128], rhs=wv_v[:, k, 0:512],
                                       start=(k == 0), stop=(k == 7))
                    for k in range(8):
                        ins = e.matmul(pp[p1][:, 0:256], lhsT=hT[:, k, tt * 128:(tt + 1) * 128], rhs=wv_v[:, k, 512:768],
                                       start=(k == 0), stop=(k == 7))
                    return ins
                S.op("pe", f, reads=[b_hTt[tt], wv_b], writes=[b_pp[p0], b_pp[p1]])
                S.op("act", (lambda tt, p0: lambda e: e.activation(out=vn[:, tt, 0:512], in_=pp[p0][:, :], func=AF.Copy))(tt, p0),
                     reads=[b_pp[p0]], writes=[b_vn[tt]])
                S.op("act", (lambda tt, p1: lambda e: e.activation(out=vn[:, tt, 512:768], in_=pp[p1][:, 0:256], func=AF.Copy))(tt, p1),
                     reads=[b_pp[p1]], writes=[b_vn[tt]])
                S.dma("sp", (lambda tt, cg: lambda e: e.dma_start(
                    out=vscr[cg * 512 + tt * 128: cg * 512 + (tt + 1) * 128, :], in_=vn[:, tt, :]))(tt, cg),
                    reads=[b_vn[tt]], writes=[b_vscr[cg][tt]], sem_buf=b_vn[tt])
                if nxt:
                    norm_b(pre_n[tt], tt, gnt, b_gnt)
                    if tt == 0:
                        pre_x.append(load_x(xc_rows(cg + 1, 3)))
                    if tt + 1 < 4:
                        pre_n[tt + 1] = norm_a(pre_x[tt + 1])
        S.op("dve", lambda e: e.tensor_scalar(out=kmb[:], in0=kmf[:], scalar1=1.0 / 256, scalar2=None, op0=ALU.mult),
             reads=[b_kmf], writes=[b_kmb])

        out_dmas = []
        for g in _DBG.get('glist', range(_DBG['ng'])):
            make_hT(lambda tt, g=g: xq[g * 512 + tt * 128: g * 512 + (tt + 1) * 128, :], 4, gnt, b_gnt)
            csi = load_cs(cosq_d, sinq_d, g * 512)
            wv_, wb_ = load_w(w_in, 8, OFF["qa"], 768, "qa")
            prev = None
            for h in range(H):
                bi = proj_fm(wv_, wb_, h * 128)
                cur = []
                rope_evac(bi, qT[:, h, :], b_qT[h], csi, cur)
                if prev is not None:
                    for fn in prev:
                        fn()
                prev = cur
            for fn in prev:
                fn()
            fillers = []
            w_ga = load_w(w_in, 8, OFF["ga"], 768, "ga")
            w_ub = load_w(w_in, 8, OFF["ub"], 768, "ub")
            w_gb = load_w(w_in, 8, OFF["gb"], 768, "gb")

            def mk_filler(wv_wb, c, dst, b_dst, func):
                def fn():
                    bi = proj_fm(wv_wb[0], wv_wb[1], c * 128)
                    S.op("act", lambda e: e.activation(out=dst, in_=pp[bi][:], func=func), reads=[b_pp[bi]], writes=[b_dst])
                return fn
            for c in range(6):
                fillers.append(mk_filler(w_ga, c, sga[:, c, :], b_sga[c], AF.Silu))
            for c in range(6):
                fillers.append(mk_filler(w_ub, c, g1[:, c, :], b_g1[c], AF.Copy))
            for c in range(6):
                fillers.append(mk_filler(w_gb, c, g2[:, c, :], b_g2[c], AF.Silu))
            for qt in range(4):
                gs = 2 * g + qt // 2

                def f(e, qt=qt):
                    ins = None
                    for h in range(H):
                        ins = e.matmul(pmisc[:, h * 32:(h + 1) * 32], lhsT=qT[:, h, qt * 128:(qt + 1) * 128],
                                       rhs=kmb[:, h, :], start=True, stop=True)
                    return ins
                S.op("pe", f, reads=b_qT + [b_kmb], writes=[b_pmisc])
                for h in range(H):
                    S.op("dve", (lambda h, gs: lambda e: e.tensor_tensor(out=bsm[:, h, :], in0=pmisc[:, h * 32:(h + 1) * 32],
                                                                         in1=vneg[:, gs, :], op=ALU.add))(h, gs),
                         reads=[b_pmisc, b_vneg], writes=[b_bsm])
                for h in range(H):
                    S.op("dve", (lambda h: lambda e: e.max(out=top8[:, h, :], in_=bsm[:, h, :]))(h),
                         reads=[b_bsm], writes=[b_top8])
                for h in range(H):
                    S.op("dve", (lambda h, gs: lambda e: e.scalar_tensor_tensor(
                        out=selv[:, h, :], in0=bsm[:, h, :], scalar=top8[:, h, 2:3], in1=vmask[:, gs, :],
                        op0=ALU.is_ge, op1=ALU.mult))(h, gs),
                        reads=[b_bsm, b_top8, b_vmask], writes=[b_selv])
                for h in range(H):
                    S.op("dve", (lambda h, gs, qt: lambda e: e.scalar_tensor_tensor(
                        out=biasb[:, qt, h, :], in0=selv[:, h, :], scalar=-NEG, in1=obias[:, gs, :],
                        op0=ALU.mult, op1=ALU.add))(h, gs, qt),
                        reads=[b_selv, b_obias], writes=[b_biasb])
                for _ in range(4):
                    if fillers:
                        fillers.pop(0)()
            while fillers:
                fillers.pop(0)()
            for h in range(H):
                half = h % 2

                def tr(e, h=h, half=half):
                    ins = None
                    for qt in range(4):
                        ins = e.transpose(out=ptr[0:32, half * 512 + qt * 128: half * 512 + (qt + 1) * 128],
                                          in_=biasb[:, qt, h, :], identity=identb[:])
                    return ins
                S.op("pe", tr, reads=[b_biasb, b_identb], writes=[b_ptr[half]])
                S.op("act", (lambda h, half: lambda e: e.activation(out=selbT[:, h, :],
                                                                    in_=ptr[0:32, half * 512:(half + 1) * 512],
                                                                    func=AF.Copy))(h, half),
                     reads=[b_ptr[half]], writes=[b_selbT[h]])
            nkt = 8 * (g + 1)
            stages = []
            for h in range(H):
                ktiles = []
                loaders = []
                for c0 in range(0, nkt, 16):
                    nt = min(16, nkt - c0)
                    ki = rot("kc", NKC)

                    def loader(ki=ki, h=h, c0=c0, nt=nt):
                        S.dma("sp", lambda e: e.dma_start(
                            out=kch[ki][:, 0:nt * 128], in_=kscr[h, :, c0 * 128:(c0 + nt) * 128]),
                            reads=[b_kscr[h][cg] for cg in range(c0 // 4, (c0 + nt + 3) // 4)], writes=[b_kch[ki]])
                        S.dma("sp", lambda e: e.dma_start(
                            out=vch[ki][:, 0:nt, 0:128],
                            in_=vscr.rearrange("(t p) c -> p t c", p=128)[:, c0:c0 + nt, h * 128:(h + 1) * 128]),
                            reads=[b_vscr[cg][t4] for cg in range(c0 // 4, (c0 + nt + 3) // 4) for t4 in range(4)],
                            writes=[b_vch[ki]])
                    loaders.append(loader)
                    for j in range(nt):
                        kt = c0 + j
                        extra = [(ohot[:, kt // 2, :], selbT[:, h, :], [b_ohot, b_selbT[h]])]
                        if kt >= nkt - 8:
                            extra.append((identb[:, :], cmask[:, kt - (nkt - 8), :], [b_identb, b_cmask]))
                        ktiles.append(dict(k=kch[ki][:, j * 128:(j + 1) * 128], kb=[b_kch[ki]],
                                           v=vch[ki][:, j, 0:128], vb=[b_vch[ki]], extra=extra, chunk=c0 // 16))
                stages.append(attn_core(ktiles, qT[:, h, :], b_qT[h],
                                        (lambda h: lambda: attn_post(sga[:, h, :], b_sga[h], oaT[:, h, :], b_oaT[h]))(h),
                                        loaders=loaders))
            run_stages(stages)

            wv_, wb_ = load_w(w_in, 8, OFF["vb"], 768, "vb")
            for tt in range(4):
                p0, p1 = 2 * (tt % 2), 2 * (tt % 2) + 1

                def f(e, tt=tt, wv_=wv_, p0=p0, p1=p1):
                    ins = None
                    for k in range(8):
                        ins = e.matmul(pp[p0][:, :], lhsT=hT[:, k, tt * 128:(tt + 1) * 128], rhs=wv_[:, k, 0:512],
                                       start=(k == 0), stop=(k == 7))
                    for k in range(8):
                        ins = e.matmul(pp[p1][:, 0:256], lhsT=hT[:, k, tt * 128:(tt + 1) * 128], rhs=wv_[:, k, 512:768],
                                       start=(k == 0), stop=(k == 7))
                    return ins
                S.op("pe", f, reads=[b_hTt[tt], wb_], writes=[b_pp[p0], b_pp[p1]])
                S.op("dve", lambda e, p0=p0, p1=p1: e.bn_stats(out=bnst[:, 0, :], in_=pp[p0][:, 0:256]), reads=[b_pp[p0]], writes=[b_bnst])
                S.op("dve", lambda e, p0=p0, p1=p1: e.bn_stats(out=bnst[:, 1, :], in_=pp[p0][:, 256:512]), reads=[b_pp[p0]], writes=[b_bnst])
                S.op("dve", lambda e, p0=p0, p1=p1: e.bn_stats(out=bnst[:, 2, :], in_=pp[p1][:, 0:256]), reads=[b_pp[p1]], writes=[b_bnst])
                si = rot("st", NST)
                st = st_t[:, si, :]
                b_st = b_st_l[si]
                S.op("dve", (lambda st: lambda e: e.bn_aggr(out=st[:, 4:6], in_=bnst[:, :, :].rearrange("p a b -> p (a b)")))(st),
                     reads=[b_bnst], writes=[b_st])
                S.op("act", (lambda st: lambda e: e.activation(out=st[:, 6:7], in_=st[:, 5:6], func=AF.Sqrt, scale=1.0, bias=EPS))(st),
                     reads=[b_st], writes=[b_st])
                S.op("dve", (lambda st: lambda e: e.reciprocal(out=st[:, 7:8], in_=st[:, 6:7]))(st), reads=[b_st], writes=[b_st])
                S.op("dve", (lambda st, p0=p0, p1=p1: lambda e: e.tensor_scalar(out=vtmp[:, 0:512], in0=pp[p0][:, :], scalar1=st[:, 4:5], scalar2=st[:, 7:8],
                                                                  op0=ALU.subtract, op1=ALU.mult))(st),
                     reads=[b_pp[p0], b_st], writes=[b_vtmp])
                S.op("dve", (lambda st, p0=p0, p1=p1: lambda e: e.tensor_scalar(out=vtmp[:, 512:768], in0=pp[p1][:, 0:256], scalar1=st[:, 4:5], scalar2=st[:, 7:8],
                                                                  op0=ALU.subtract, op1=ALU.mult))(st),
                     reads=[b_pp[p1], b_st], writes=[b_vtmp])
                S.op("pool", (lambda tt: lambda e: e.tensor_tensor(out=vn[:, tt, :], in0=vtmp[:, :], in1=lng[:, :], op=ALU.mult))(tt),
                     reads=[b_vtmp, b_lng], writes=[b_vn[tt]])
            for c in range(6):
                bi = rot("pp", 2)

                def f(e, bi=bi, c=c):
                    ins = None
                    for tt in range(4):
                        ins = e.matmul(pp[bi][:, tt * 128:(tt + 1) * 128], lhsT=vn[:, tt, c * 128:(c + 1) * 128],
                                       rhs=wspT[:, c, :], start=True, stop=True)
                    return ins
                S.op("pe", f, reads=b_vn + [b_wspT], writes=[b_pp[bi]])
                S.op("dve", (lambda bi, c: lambda e: e.tensor_tensor(
                    out=tmpb[:, :].rearrange("p (a b) -> p a b", b=128),
                    in0=pp[bi][:, :].rearrange("p (a b) -> p a b", b=128),
                    in1=bspc[c][:, :].unsqueeze(1).broadcast_to([128, 4, 128]), op=ALU.add))(bi, c),
                    reads=[b_pp[bi], b_bspc[c]], writes=[b_tmpb])
                S.op("pool", (lambda c: lambda e: e.tensor_tensor(out=usg[:, :], in0=g1[:, c, :], in1=g2[:, c, :], op=ALU.mult))(c),
                     reads=[b_g1[c], b_g2[c]], writes=[b_usg])
                S.op("dve", (lambda c: lambda e: e.tensor_tensor(out=g3[:, c, :], in0=tmpb[:, :], in1=usg[:, :], op=ALU.mult))(c),
                     reads=[b_tmpb, b_usg], writes=[b_g3[c]])

            def merge(first, last, moff, wsrc, kcn, srcT, b_src, wname):
                wv3, wb3 = load_w(wsrc, kcn, 0, D, wname)
                for half in range(2):
                    wv2, wb2 = load_w(w_in, 8, moff + half * 512, 512, "m%d_%d" % (moff, half))
                    for c in range(4):
                        bi = proj_fm(wv2, wb2, c * 128)
                        S.op("act", (lambda bi, c: lambda e: e.activation(out=smg[:, c, :], in_=pp[bi][:], func=AF.Sigmoid))(bi, c),
                             reads=[b_pp[bi]], writes=[b_smg[c]])
                    for c in range(4):
                        oc = half * 4 + c
                        bi = proj_fm(wv3, wb3, oc * 128, kcn=kcn, rhs_fn=lambda k: srcT[:, k, :], rhs_bufs=b_src[0:kcn])
                        if first:
                            S.op("dve", (lambda bi, oc, c: lambda e: e.tensor_tensor(out=yacc[:, oc, :], in0=pp[bi][:], in1=smg[:, c, :],
                                                                                     op=ALU.mult))(bi, oc, c),
                                 reads=[b_pp[bi], b_smg[c]], writes=[b_yacc[oc]])
                        else:
                            S.op("dve", (lambda bi, c: lambda e: e.tensor_tensor(out=ytmp[:, :], in0=pp[bi][:], in1=smg[:, c, :],
                                                                                 op=ALU.mult))(bi, c),
                                 reads=[b_pp[bi], b_smg[c]], writes=[b_ytmp])
                            S.op("dve", (lambda oc: lambda e: e.tensor_tensor(out=yacc[:, oc, :], in0=yacc[:, oc, :], in1=ytmp[:, :],
                                                                              op=ALU.add))(oc),
                                 reads=[b_yacc[oc], b_ytmp], writes=[b_yacc[oc]])

            merge(True, False, OFF["mb"], w_b, 6, g3, b_g3, "w_b")

            wv_, wb_ = load_w(w_in, 8, OFF["qc"], 512, "qc")
            for c in range(HC):
                bi = proj_fm(wv_, wb_, c * 128)
                S.op("act", (lambda bi, c: lambda e: e.activation(out=g1[:, c, :], in_=pp[bi][:], func=AF.Copy))(bi, c),
                     reads=[b_pp[bi]], writes=[b_g1[c]])
            wv_, wb_ = load_w(w_in, 8, OFF["gc"], 512, "gc")
            for c in range(HC):
                bi = proj_fm(wv_, wb_, c * 128)
                S.op("act", (lambda bi, c: lambda e: e.activation(out=g2[:, c, :], in_=pp[bi][:], func=AF.Silu))(bi, c),
                     reads=[b_pp[bi]], writes=[b_g2[c]])
            stages = []
            for hc in range(HC):
                ktiles = [dict(k=mkT[:, hc, mt * 128:(mt + 1) * 128], kb=[b_mkT], v=mvt[:, mt, hc, 0:128], vb=[b_mvt], extra=[])
                          for mt in range(2)]
                stages.append(attn_core(ktiles, g1[:, hc, :], b_g1[hc],
                                        (lambda hc: lambda: attn_post(g2[:, hc, :], b_g2[hc], g3[:, hc, :], b_g3[hc]))(hc)))
            run_stages(stages)
            merge(False, False, OFF["mc"], w_c, 4, g3, b_g3, "w_c")
            merge(False, True, OFF["ma"], w_a, 6, oaT, b_oaT, "w_a")

            wo0, wob0 = load_w(w_o, 8, 0, 512, "wo0")
            wo1, wob1 = load_w(w_o, 8, 512, 512, "wo1")
            for tt in range(4):
                xi = rot("xt", NXT)
                S.dma("sp", (lambda xi, tt, g: lambda e: e.dma_start(
                    out=xt[xi][:], in_=xq[g * 512 + tt * 128: g * 512 + (tt + 1) * 128, :]))(xi, tt, g),
                    writes=[b_xt[xi]])

                p0, p1 = 2 * (tt % 2), 2 * (tt % 2) + 1

                def f(e, tt=tt, wo0=wo0, wo1=wo1, p0=p0, p1=p1):
                    ins = None
                    for k in range(8):
                        ins = e.matmul(pp[p0][:, :], lhsT=yT[:, k, tt * 128:(tt + 1) * 128], rhs=wo0[:, k, :],
                                       start=(k == 0), stop=(k == 7))
                    for k in range(8):
                        ins = e.matmul(pp[p1][:, :], lhsT=yT[:, k, tt * 128:(tt + 1) * 128], rhs=wo1[:, k, :],
                                       start=(k == 0), stop=(k == 7))
                    return ins
                S.op("pe", f, reads=b_yT + [wob0, wob1], writes=[b_pp[p0], b_pp[p1]])
                ri = xi
                S.op("dve", (lambda ri, xi, p0=p0, p1=p1: lambda e: e.tensor_tensor(out=xt[ri][:, 0:512], in0=pp[p0][:, :], in1=xt[xi][:, 0:512],
                                                                      op=ALU.add))(ri, xi),
                     reads=[b_pp[p0], b_xt[xi]], writes=[b_xt[ri]])
                S.op("dve", (lambda ri, xi, p0=p0, p1=p1: lambda e: e.tensor_tensor(out=xt[ri][:, 512:1024], in0=pp[p1][:, :], in1=xt[xi][:, 512:1024],
                                                                      op=ALU.add))(ri, xi),
                     reads=[b_pp[p1], b_xt[xi]], writes=[b_xt[ri]])
                si = rot("st", NST)
                st = st_t[:, si, :]
                b_st = b_st_l[si]
                S.op("act", (lambda ri, st: lambda e: e.activation(out=junk[:], in_=xt[ri][:], func=AF.Square, accum_out=st[:, 0:1]))(ri, st),
                     reads=[b_xt[ri]], writes=[b_junk, b_st])
                S.op("act", (lambda st: lambda e: e.activation(out=st[:, 2:3], in_=st[:, 0:1], func=AF.Sqrt, scale=1.0 / D, bias=EPS))(st),
                     reads=[b_st], writes=[b_st])
                S.op("dve", (lambda st: lambda e: e.reciprocal(out=st[:, 3:4], in_=st[:, 2:3]))(st), reads=[b_st], writes=[b_st])
                S.op("dve", (lambda ri, st: lambda e: e.scalar_tensor_tensor(out=xt[ri][:], in0=xt[ri][:], scalar=st[:, 3:4], in1=fing[:],
                                                                             op0=ALU.mult, op1=ALU.mult))(ri, st),
                     reads=[b_xt[ri], b_st, b_fing], writes=[b_xt[ri]])
                od = S.dma("sp", (lambda ri, tt, g: lambda e: e.dma_start(
                    out=out_d[g * 512 + tt * 128: g * 512 + (tt + 1) * 128, :], in_=xt[ri][:]))(ri, tt, g),
                    reads=[b_xt[ri]], writes=[Buf("o")], sem_buf=b_xt[ri])
                out_dmas.append(od)

        S.finalize(out_dmas[-2:])
    return nc


def _own_blocks(r):
    blocks = []
    for g in range(NG):
        blocks.append(4 * g + (0 if r == 0 else 1))
        blocks.append(4 * g + (3 if r == 0 else 2))
    return blocks


def _host_tables(r):
    blocks = _own_blocks(r)
    half = 8
    inv_freq = (500000.0 ** (-np.arange(0, 32, 2, dtype=np.float32) / 32)).astype(np.float32)
    qpos = np.concatenate([np.arange(b * 256, (b + 1) * 256) for b in blocks])

    def cs(pos):
        ang = pos.astype(np.float32)[:, None] * inv_freq[None, :]
        c = np.cos(ang).astype(np.float32).T
        s = np.sin(ang).astype(np.float32).T
        return np.ascontiguousarray(np.concatenate([c, c], 0)), np.ascontiguousarray(np.concatenate([s, s], 0))
    cosq, sinq = cs(qpos)
    cosc, sinc = cs(np.arange(S_LEN))
    cm = np.zeros((128, 8, 512), np.float32)
    for kt in range(8):
        kpos = kt * 128 + np.arange(128)
        for slot in range(2):
            rel = blocks[slot] * 256 + np.arange(256)
            cm[:, kt, slot * 256:(slot + 1) * 256] = np.where(kpos[:, None] <= rel[None, :], 0.0, NEG)
    vneg = np.zeros((128, 16, 32), np.float32)
    vmask = np.zeros((128, 16, 32), np.float32)
    obias = np.full((128, 16, 32), NEG, np.float32)
    for i, b in enumerate(blocks):
        vmask[:, i, :b] = 1.0
        vneg[:, i, b:] = -1e30
        obias[:, i, b] = 0.0
    return dict(cosq=cosq, sinq=sinq, cosc=cosc, sinc=sinc, cmask=cm.reshape(128, -1),
                vneg=vneg.reshape(128, -1), vmask=vmask.reshape(128, -1), obias=obias.reshape(128, -1))


_NC_CACHE = {}


def kernel(x, mem, norm_g, mem_norm_g, final_norm_g, w_in, w_mem_kv, gmlp_ln_g, w_spatial, b_spatial,
           w_branch_a, w_branch_b, w_branch_c, w_out):
    f = lambda a: np.ascontiguousarray(np.asarray(a, dtype=np.float32))
    x = f(x); mem = f(mem)
    if "nc" not in _NC_CACHE:
        _NC_CACHE["nc"] = build_program()
    nc = _NC_CACHE["nc"]
    rm = np.zeros((32, 32), np.float32)
    for i in range(16):
        rm[i + 16, i] = -1.0
        rm[i, i + 16] = 1.0
    ohot = np.zeros((32, 32, 128), np.float32)
    for j in range(32):
        ohot[j, j, :] = 1.0
    common = dict(
        w_in=f(w_in[0]), w_mkv=f(w_mem_kv[0]), w_a=f(w_branch_a[0]), w_b=f(w_branch_b[0]), w_c=f(w_branch_c[0]),
        w_o=f(w_out[0]),
        gnt=f(np.asarray(norm_g[0]).reshape(8, 128).T), gmt=f(np.asarray(mem_norm_g[0]).reshape(8, 128).T),
        fing=f(np.asarray(final_norm_g).reshape(1, D)), lng=f(np.asarray(gmlp_ln_g[0]).reshape(1, 768)),
        wsp=f(w_spatial[0]), bsp=f(np.asarray(b_spatial[0]).reshape(6, 1, 128)),
        ident=np.eye(128, dtype=np.float32), tril=np.tril(np.ones((128, 128), np.float32)),
        rm=rm, ohot=ohot.reshape(32, -1),
    )
    tabs = [_host_tables(0), _host_tables(1)]
    in_maps = []
    rows = []
    for c in range(8):
        b, r = c // 2, c % 2
        blocks = _own_blocks(r)
        idx = np.concatenate([np.arange(bk * 256, (bk + 1) * 256) for bk in blocks])
        rows.append(idx)
        m = dict(common)
        m.update(tabs[r])
        m["xq"] = np.ascontiguousarray(x[b][idx])
        m["xc"] = x[b]
        m["memx"] = mem[b]
        in_maps.append(m)
    ncr = _DBG['cores']
    resu = run_bass_kernel_spmd(nc, in_maps[:ncr], core_ids=list(range(ncr)))
    out = np.zeros((4, S_LEN, D), np.float32)
    for c in range(ncr):
        out[c // 2][rows[c]] = np.asarray(resu.results[c]["y_out"], dtype=np.float32)
    return out
```
